# Optimizing a Trainium2 kernel written in Bass

```python
import math
import jax, jax.numpy as jnp
from jax import lax
import numpy as np

D_MODEL = 1024
BATCH = 8
SEQ = 8192
DEPTH = 4

D_MIX = D_MODEL
DN_HEADS = 4
DN_HEAD_DIM = 128
DN_WIDTH = DN_HEADS * DN_HEAD_DIM
DN_CONV = 4
CHUNK = 64
SC_WIDTH = D_MIX - DN_WIDTH
SC_GROUPS = 4
SC_GROUP_DIM = SC_WIDTH // SC_GROUPS
SC_CONV = 3
D_FF = ((8 * D_MODEL + 3 * 256 - 1) // (3 * 256)) * 256
W_IN_COLS = 4 * DN_WIDTH + 2 * DN_HEADS + 3 * SC_WIDTH
EPS = 1e-6

kernel_name = 'hybrid_gdn_shortconv_swiglu'


def rms_norm(x, gain):
    xf = x.astype(jnp.float32)
    y = xf * lax.rsqrt(jnp.mean(xf * xf, axis=-1, keepdims=True) + EPS)
    return (y * gain.astype(jnp.float32)).astype(x.dtype)


def l2_normalize(x):
    return x * lax.rsqrt(jnp.sum(x * x, axis=-1, keepdims=True) + EPS)


def causal_depthwise_conv(x, w):
    K = w.shape[0]
    T = x.shape[1]
    xp = jnp.pad(x, ((0, 0), (K - 1, 0), (0, 0)))
    y = xp[:, 0:T, :] * w[0]
    for j in range(1, K):
        y = y + xp[:, j:j + T, :] * w[j]
    return y


def chunk_gated_delta_rule(q, k, v, g, beta):
    Bsz, T, H, DK = q.shape
    DV = v.shape[-1]
    C = CHUNK
    N = T // C
    q = q * (DK ** -0.5)

    def to_chunks(t):
        return t.reshape(Bsz, N, C, H, t.shape[-1]).transpose(0, 3, 1, 2, 4)

    q, k, v = to_chunks(q), to_chunks(k), to_chunks(v)
    g = jnp.cumsum(g.reshape(Bsz, N, C, H).transpose(0, 3, 1, 2), axis=-1)
    beta = beta.reshape(Bsz, N, C, H).transpose(0, 3, 1, 2)

    causal = jnp.tril(jnp.ones((C, C), dtype=bool))
    strict = jnp.tril(jnp.ones((C, C), dtype=bool), -1)
    decay = jnp.exp(jnp.where(causal, g[..., :, None] - g[..., None, :], -jnp.inf))

    k_beta = k * beta[..., None]
    v_beta = v * beta[..., None]
    lower = jnp.where(strict, jnp.einsum('bhncd,bhnmd->bhncm', k_beta, k) * decay, 0.0)
    a_mat = lower + jnp.eye(C, dtype=jnp.float32)
    rhs = jnp.concatenate([v_beta, k_beta * jnp.exp(g)[..., None]], axis=-1)
    sol = lax.linalg.triangular_solve(a_mat, rhs, left_side=True, lower=True, unit_diagonal=True)
    u, w = sol[..., :DV], sol[..., DV:]

    qk = jnp.einsum('bhncd,bhnmd->bhncm', q, k) * decay
    q_dec = q * jnp.exp(g)[..., None]
    k_dec = k * jnp.exp(g[..., -1:] - g)[..., None]
    g_last = jnp.exp(g[..., -1])

    def step(S, xs):
        qk_i, q_dec_i, k_dec_i, u_i, w_i, gl_i = xs
        v_new = u_i - jnp.einsum('bhck,bhkv->bhcv', w_i, S)
        o_i = jnp.einsum('bhck,bhkv->bhcv', q_dec_i, S) + jnp.einsum('bhcm,bhmv->bhcv', qk_i, v_new)
        S = S * gl_i[..., None, None] + jnp.einsum('bhck,bhcv->bhkv', k_dec_i, v_new)
        return S, o_i

    xs = (jnp.moveaxis(qk, 2, 0), jnp.moveaxis(q_dec, 2, 0), jnp.moveaxis(k_dec, 2, 0),
          jnp.moveaxis(u, 2, 0), jnp.moveaxis(w, 2, 0), jnp.moveaxis(g_last, 2, 0))
    S0 = jnp.zeros((Bsz, H, DK, DV), dtype=jnp.float32)
    _, o = lax.scan(step, S0, xs)
    return o.transpose(1, 0, 3, 2, 4).reshape(Bsz, T, H, DV)


def hybrid_layer(x, norm1_g, w_in, dn_conv_w, dn_a_log, dn_dt_bias, dn_norm_g,
                 sc_conv_w, sc_norm_g, w_out, norm2_g, ffn_w_gate, ffn_w_up, ffn_w_down):
    Bsz, T, _ = x.shape
    h = rms_norm(x, norm1_g)
    proj = h @ w_in
    o1 = 3 * DN_WIDTH
    o2 = o1 + DN_WIDTH
    o3 = o2 + DN_HEADS
    o4 = o3 + DN_HEADS
    qkv, z, b_in, a_in, sc_in = proj[..., :o1], proj[..., o1:o2], proj[..., o2:o3], proj[..., o3:o4], proj[..., o4:]

    qkv = jax.nn.silu(causal_depthwise_conv(qkv, dn_conv_w)).astype(jnp.float32)
    q = l2_normalize(qkv[..., :DN_WIDTH].reshape(Bsz, T, DN_HEADS, DN_HEAD_DIM))
    k = l2_normalize(qkv[..., DN_WIDTH:2 * DN_WIDTH].reshape(Bsz, T, DN_HEADS, DN_HEAD_DIM))
    v = qkv[..., 2 * DN_WIDTH:].reshape(Bsz, T, DN_HEADS, DN_HEAD_DIM)
    beta = jax.nn.sigmoid(b_in.astype(jnp.float32))
    g = -jnp.exp(dn_a_log.astype(jnp.float32)) * jax.nn.softplus(
        a_in.astype(jnp.float32) + dn_dt_bias.astype(jnp.float32))
    o_dn = chunk_gated_delta_rule(q, k, v, g, beta)
    zf = z.astype(jnp.float32).reshape(Bsz, T, DN_HEADS, DN_HEAD_DIM)
    o_dn = (o_dn * lax.rsqrt(jnp.mean(o_dn * o_dn, axis=-1, keepdims=True) + EPS)
            * dn_norm_g.astype(jnp.float32) * jax.nn.silu(zf))
    o_dn = o_dn.reshape(Bsz, T, DN_WIDTH).astype(x.dtype)

    gate_b, gate_c, hv = sc_in[..., :SC_WIDTH], sc_in[..., SC_WIDTH:2 * SC_WIDTH], sc_in[..., 2 * SC_WIDTH:]
    y = gate_b * causal_depthwise_conv(gate_c * hv, sc_conv_w)
    yf = y.astype(jnp.float32).reshape(Bsz, T, SC_GROUPS, SC_GROUP_DIM)
    yf = yf * lax.rsqrt(jnp.mean(yf * yf, axis=-1, keepdims=True) + EPS)
    o_sc = (yf * sc_norm_g.astype(jnp.float32).reshape(SC_GROUPS, SC_GROUP_DIM)).reshape(Bsz, T, SC_WIDTH).astype(x.dtype)

    x = x + jnp.concatenate([o_dn, o_sc], axis=-1) @ w_out

    h2 = rms_norm(x, norm2_g)
    x = x + (jax.nn.silu(h2 @ ffn_w_gate) * (h2 @ ffn_w_up)) @ ffn_w_down
    return x


def setup_inputs(seed: int = 0) -> dict:
    key = jax.random.key(seed)
    ks = jax.random.split(key, 16)
    f32 = jnp.float32

    def nrm(k, shape, scale):
        return jax.random.normal(k, shape, f32) * scale

    def gain(k, shape):
        return 1.0 + 0.02 * jax.random.normal(k, shape, f32)

    x = nrm(ks[0], (BATCH, SEQ, D_MODEL), 1.0)
    norm1_g = gain(ks[1], (DEPTH, D_MODEL))
    w_in = nrm(ks[2], (DEPTH, D_MODEL, W_IN_COLS), D_MODEL ** -0.5)
    dn_conv_w = nrm(ks[3], (DEPTH, DN_CONV, 3 * DN_WIDTH), DN_CONV ** -0.5)
    dn_a_log = jnp.log(jax.random.uniform(ks[4], (DEPTH, DN_HEADS), f32, 1.0, 16.0))
    dt = jnp.exp(jax.random.uniform(ks[5], (DEPTH, DN_HEADS), f32, math.log(1e-3), math.log(1e-1)))
    dn_dt_bias = dt + jnp.log(-jnp.expm1(-dt))
    dn_norm_g = gain(ks[6], (DEPTH, DN_HEAD_DIM))
    sc_conv_w = nrm(ks[7], (DEPTH, SC_CONV, SC_WIDTH), SC_CONV ** -0.5)
    sc_norm_g = gain(ks[8], (DEPTH, SC_WIDTH))
    w_out = nrm(ks[9], (DEPTH, D_MIX, D_MODEL), D_MIX ** -0.5)
    norm2_g = gain(ks[10], (DEPTH, D_MODEL))
    ffn_w_gate = nrm(ks[11], (DEPTH, D_MODEL, D_FF), D_MODEL ** -0.5)
    ffn_w_up = nrm(ks[12], (DEPTH, D_MODEL, D_FF), D_MODEL ** -0.5)
    ffn_w_down = nrm(ks[13], (DEPTH, D_FF, D_MODEL), D_FF ** -0.5)
    final_norm_g = gain(ks[14], (D_MODEL,))
    return {'x': x, 'norm1_g': norm1_g, 'w_in': w_in, 'dn_conv_w': dn_conv_w,
            'dn_a_log': dn_a_log, 'dn_dt_bias': dn_dt_bias, 'dn_norm_g': dn_norm_g,
            'sc_conv_w': sc_conv_w, 'sc_norm_g': sc_norm_g, 'w_out': w_out,
            'norm2_g': norm2_g, 'ffn_w_gate': ffn_w_gate, 'ffn_w_up': ffn_w_up,
            'ffn_w_down': ffn_w_down, 'final_norm_g': final_norm_g}


def reference(x, norm1_g, w_in, dn_conv_w, dn_a_log, dn_dt_bias, dn_norm_g, sc_conv_w,
              sc_norm_g, w_out, norm2_g, ffn_w_gate, ffn_w_up, ffn_w_down, final_norm_g):
    for l in range(DEPTH):
        x = hybrid_layer(x, norm1_g[l], w_in[l], dn_conv_w[l], dn_a_log[l], dn_dt_bias[l],
                         dn_norm_g[l], sc_conv_w[l], sc_norm_g[l], w_out[l], norm2_g[l],
                         ffn_w_gate[l], ffn_w_up[l], ffn_w_down[l])
    return rms_norm(x, final_norm_g)
```

```python
import numpy as np
import concourse.bass as bass
import concourse.mybir as mybir
from concourse.bass_utils import run_bass_kernel_spmd

F32 = mybir.dt.float32
BF16 = mybir.dt.bfloat16
AF = mybir.ActivationFunctionType
ALU = mybir.AluOpType

D_MODEL = 1024
KC = 8
DEPTH = 4
SEQ = 8192
BATCH = 8
H = 4
DFF = 2816
FC = 22
W_IN_COLS = 3592
NT = 512
NCH = NT // 128
BIG = 128.0
EPS = 1e-6
NSLOT = 3
HFS = (8, 8, 6)
SLOT_ELEMS = 4096

ENGS = ("pe", "act", "dve", "pool", "sp")
TAG_OPS = False


class T:
    __slots__ = ("ap", "w", "r", "name")

    def __init__(self, ap, name=""):
        self.ap = ap
        self.w = None
        self.r = []
        self.name = name


class Op:
    __slots__ = ("eng", "fn", "deps", "is_dma", "sem", "semval", "needs_inc", "incval", "tag")

    def __init__(self, eng, fn, deps, is_dma=False):
        self.eng = eng
        self.fn = fn
        self.deps = deps
        self.is_dma = is_dma
        self.sem = None
        self.semval = 0
        self.needs_inc = False
        self.incval = 0


class Prog:
    def __init__(self, nc):
        self.nc = nc
        self.ops = {e: [] for e in ENGS}
        self.order = []
        self.dma_sems = []

    def new_dma_sem(self, name):
        h = self.nc.alloc_semaphore(name)
        self.dma_sems.append([h, 0])
        return len(self.dma_sems) - 1

    def op(self, eng, fn, reads=(), writes=(), dma_sem=None):
        deps = []
        for t in reads:
            if t.w is not None:
                deps.append(t.w)
        for t in writes:
            if t.w is not None:
                deps.append(t.w)
            deps.extend(t.r)
        o = Op(eng, fn, deps, is_dma=dma_sem is not None)
        o.tag = None
        if TAG_OPS:
            import sys as _sys
            f_ = _sys._getframe(1)
            names = []
            while f_ is not None and len(names) < 6:
                nm_ = f_.f_code.co_name
                if nm_ not in ("mm", "hmm", "tt", "ts", "stt", "cp", "act", "tr", "op", "proj_chunk", "<lambda>"):
                    names.append(nm_)
                f_ = f_.f_back
            o.tag = "/".join(names[:2])
        if dma_sem is not None:
            ent = self.dma_sems[dma_sem]
            ent[1] += 16
            o.sem = ent[0]
            o.semval = ent[1]
        for t in reads:
            t.r.append(o)
        for t in writes:
            t.w = o
            t.r = []
        self.ops[eng].append(o)
        self.order.append(o)
        return o

    def emit(self):
        nc = self.nc
        for o in self.order:
            for d in o.deps:
                if d.is_dma:
                    continue
                if d.eng != o.eng or o.eng != "pe":
                    d.needs_inc = True
        prog_sem = {e: nc.alloc_semaphore("prog_" + e) for e in ENGS}
        for e in ENGS:
            c = 0
            for o in self.ops[e]:
                if o.needs_inc and not o.is_dma:
                    c += 1
                    o.incval = c
        engobj = {"pe": nc.tensor, "act": nc.scalar, "dve": nc.vector, "pool": nc.gpsimd, "sp": nc.sync}

        def run_engine(e):
            eng = engobj[e]
            waited = {}
            for o in self.ops[e]:
                need = {}
                for d in o.deps:
                    if d.is_dma:
                        key = ("d", id(d.sem))
                        if need.get(key, (None, 0))[1] < d.semval:
                            need[key] = (d.sem, d.semval)
                    elif d.eng != e or e != "pe":
                        key = ("e", d.eng)
                        if need.get(key, (None, 0))[1] < d.incval:
                            need[key] = (prog_sem[d.eng], d.incval)
                for key, (sem, val) in need.items():
                    if waited.get(key, 0) < val:
                        eng.wait_ge(sem, val)
                        waited[key] = val
                ins = o.fn()
                if o.tag is not None:
                    ins.annotate(o.tag)
                if o.is_dma:
                    ins.then_inc(o.sem, 16)
                elif o.needs_inc:
                    ins.then_inc(prog_sem[e], 1)
            return waited

        with nc.Block() as block:
            @block.tensor
            def _(t):
                run_engine("pe")

            @block.scalar
            def _(a):
                run_engine("act")

            @block.vector
            def _(v):
                run_engine("dve")

            @block.gpsimd
            def _(g):
                run_engine("pool")

            @block.sync
            def _(s):
                w = run_engine("sp")
                for h, cnt in self.dma_sems:
                    if cnt > 0 and w.get(("d", id(h)), 0) < cnt:
                        nc.sync.wait_ge(h, cnt)


class Pool_:
    def __init__(self, tiles):
        self.tiles = tiles
        self.i = 0

    def get(self):
        t = self.tiles[self.i % len(self.tiles)]
        self.i += 1
        return t


def panel_specs():
    specs = []
    for nm in ("q", "k", "v", "z"):
        specs.append(("in_" + nm, KC, 512, 0))
    for g in range(4):
        specs.append(("in_sc%d" % g, KC, 384, 0))
    specs.append(("out0", KC, 512, 0))
    specs.append(("out1", KC, 512, 0))
    for p in range(11):
        specs.append(("gu%d" % p, KC, 512, 1))
    for hf in range(len(HFS)):
        for pn in range(2):
            specs.append(("dn%d_%d" % (hf, pn), HFS[hf], 512, 1))
    return specs


def build_program(seq=SEQ, depth=DEPTH, debug_out=None):
    ntiles = seq // NT
    nc = bass.Bass("TRN2", target_bir_lowering=False)
    P = Prog(nc)

    x_d = nc.dram_tensor("x", [seq, D_MODEL], F32, kind="ExternalInput").ap()
    out_d = nc.dram_tensor("out", [seq, D_MODEL], F32, kind="ExternalOutput").ap()
    w_in_d = nc.dram_tensor("w_in", [depth, D_MODEL, W_IN_COLS], F32, kind="ExternalInput").ap()
    w_out_d = nc.dram_tensor("w_out", [depth, D_MODEL, D_MODEL], F32, kind="ExternalInput").ap()
    wg_d = nc.dram_tensor("ffn_w_gate", [depth, D_MODEL, DFF], F32, kind="ExternalInput").ap()
    wu_d = nc.dram_tensor("ffn_w_up", [depth, D_MODEL, DFF], F32, kind="ExternalInput").ap()
    wd_d = nc.dram_tensor("ffn_w_down", [depth, DFF, D_MODEL], F32, kind="ExternalInput").ap()
    NPAR = depth * (8 + 8 + 48 + 12 + 1 + 4 + 4 + 4) + 8
    par_d = nc.dram_tensor("params", [128, NPAR], F32, kind="ExternalInput").ap()
    NCONST = 128 * 5 + 512 + 512
    cst_d = nc.dram_tensor("consts", [128, NCONST], F32, kind="ExternalInput").ap()

    specs = panel_specs()
    wscr = []
    for l in range(depth):
        row = []
        for (nm, kch, ncols, _rg) in specs:
            d = nc.dram_tensor("ws_%d_%s" % (l, nm), [128, kch * ncols], BF16, kind="Internal").ap()
            row.append(T(d, "ws_%d_%s" % (l, nm)))
        wscr.append(row)

    def sb(name, shape, dt):
        return nc.alloc_sbuf_tensor(name, shape, dt)

    par_sb = sb("par", [128, NPAR], F32)
    cst_sb = sb("cst", [128, 640], F32)
    par_t = T(par_sb, "par")
    cst_t = T(cst_sb, "cst")
    identf = cst_sb[:, 0:128]
    trif = cst_sb[:, 128:256]
    smf = cst_sb[:, 256:384]
    bigif = cst_sb[:, 384:512]
    onesf = cst_sb[:, 512:640]
    cstb_sb = sb("cstb", [128, 256 + 512 + 512], BF16)
    cstb_t = T(cstb_sb, "cstb")
    identb = cstb_sb[:, 0:128]
    onesb = cstb_sb[:, 128:256]
    i4b = cstb_sb[:, 256:768]
    bd4b = cstb_sb[:, 768:1280]
    misc_sb = sb("misc", [128, 8], F32)
    misc_t = T(misc_sb, "misc")

    def poff(l):
        return l * 89
    OFF_G1, OFF_G2, OFF_DNC, OFF_SCC, OFF_DNG, OFF_SCG, OFF_ALOG, OFF_DTB = 0, 8, 16, 64, 76, 77, 81, 85
    OFF_GF = depth * 89

    NL = 2
    xresL_sb = [sb("xres_%d" % i, [128, KC, NT], F32) for i in range(NL)]
    xresL = [[T(xresL_sb[i][:, k, :], "xres%d_%d" % (i, k)) for k in range(KC)] for i in range(NL)]
    hTL_sb = [sb("hT_%d" % i, [128, KC, NT], BF16) for i in range(NL)]
    hTL = [[T(hTL_sb[i][:, k, :], "hT%d_%d" % (i, k)) for k in range(KC)] for i in range(NL)]
    qkv_sb = sb("qkvT", [128, 12, NT], BF16)
    qkv = [T(qkv_sb[:, k, :], "qkv%d" % k) for k in range(12)]
    sz_sb = sb("szT", [128, 4, NT], BF16)
    sz = [T(sz_sb[:, k, :], "sz%d" % k) for k in range(4)]
    ocat_sb = qkv_sb
    ocat = qkv[0:8]
    HF = HFS[0]
    act_sb = sb("actT", [128, HF, NT], BF16)
    actT = [T(act_sb[:, k, :], "act%d" % k) for k in range(HF)]
    wslot_sb = [[sb("wslot%d_%d" % (r, i), [128, SLOT_ELEMS], BF16) for i in range(NSLOT)] for r in range(2)]
    wslot = [[T(wslot_sb[r][i], "wslot%d_%d" % (r, i)) for i in range(NSLOT)] for r in range(2)]
    wslot_sem = [[P.new_dma_sem("wsl%d_%d" % (r, i)) for i in range(NSLOT)] for r in range(2)]
    wbd_sb = sb("wbd", [128, depth * KC * 8], BF16)
    wbd_t = T(wbd_sb, "wbd")
    S_sb = [sb("S%d" % l, [128, 512], F32) for l in range(depth)]
    S_t = [T(S_sb[l], "S%d" % l) for l in range(depth)]
    Sb_one = sb("Sb", [128, 512], BF16)
    Sb_one_t = T(Sb_one, "Sb")
    Sb_sb = [Sb_one for l in range(depth)]
    Sb_t = [Sb_one_t for l in range(depth)]
    hist_sb = [sb("hist%d" % l, [128, 12 * 3 + 4 * 2], BF16) for l in range(depth)]
    hist_t = [T(hist_sb[l], "hist%d" % l) for l in range(depth)]
    negA_sb = sb("negA", [128, depth * 4], F32)
    negA_t = T(negA_sb, "negA")
    NXB = 1
    NXI = 2
    xin_sb = [sb("xin%d" % i, [128, 512], F32) for i in range(NXI)]
    xin = Pool_([T(xin_sb[i], "xin%d" % i) for i in range(NXI)])
    xin_sem = [P.new_dma_sem("xin%d" % i) for i in range(NXI)]
    xout_sb = [sb("xout%d" % i, [128, 512], F32) for i in range(NXB)]
    xout = Pool_([T(xout_sb[i], "xout%d" % i) for i in range(NXB)])
    xout_sem = [P.new_dma_sem("xout%d" % i) for i in range(NXB)]
    NF = 5
    f32s_sb = [sb("f32s%d" % i, [128, 512], F32) for i in range(NF)]
    f32s = Pool_([T(f32s_sb[i], "f32s%d" % i) for i in range(NF)])
    f32f_sb = [sb("f32f%d" % i, [128, 512], F32) for i in range(1)]
    f32f = Pool_([T(f32f_sb[i], "f32f%d" % i) for i in range(1)])
    def bpool(name, n):
        hs = [sb("%s%d" % (name, i), [128, 512], BF16) for i in range(n)]
        return Pool_([T(hs[i], "%s%d" % (name, i)) for i in range(n)])
    b16s = bpool("b16s", 6)
    b16f = bpool("b16f", 2)
    kdec_p = bpool("kdec", 2)
    qkT_p = bpool("qkT", 2)
    qdec_p = bpool("qdec", 2)
    chain_p = bpool("chain", 8)
    dec_p = bpool("dec", 6)
    wT_p = bpool("wT", 2)
    r_p = bpool("rp", 4)
    boff_p = bpool("boff", 2)
    ab_p = bpool("ab", 8)
    diag_p = bpool("diag", 3)
    stg_sb = [sb("stg%d" % i, [128, 520], BF16) for i in range(3)]
    stg = Pool_([T(stg_sb[i], "stg%d" % i) for i in range(3)])
    small_sb = [sb("small%d" % i, [128, 64], F32) for i in range(8)]
    smalls = Pool_([T(small_sb[i], "small%d" % i) for i in range(8)])
    tm_sb = [sb("tm%d" % i, [128, 16], F32) for i in range(8)]
    tm = [T(tm_sb[i], "tm%d" % i) for i in range(8)]
    ps_h = [nc.alloc_psum_tensor("ps%d" % i, [128, 512], F32) for i in range(8)]
    psum_m = Pool_([T(ps_h[i], "ps%d" % i) for i in range(5)])
    psum_f = Pool_([T(ps_h[i], "ps%d" % i) for i in range(5, 8)])
    MP = (psum_m, f32s, b16s)
    FP = (psum_f, f32f, b16f)

    dbg_names = []

    def dbg(name, ts_, ap):
        if debug_out is None or name in dbg_names:
            return
        shp = list(ap.shape)
        n = 1
        for v_ in shp[1:]:
            n *= v_
        d = nc.dram_tensor("dbg_" + name, shp, ap.dtype, kind="ExternalOutput").ap()
        sem = P.new_dma_sem("dbg_" + name)
        P.op("sp", lambda: nc.sync.dma_start(out=d, in_=ap), reads=ts_, dma_sem=sem)
        dbg_names.append(name)

    def mm(out_t, out_ap, lhsT, rhs, start, stop, reads):
        return P.op("pe", lambda: nc.tensor.matmul(out_ap, lhsT, rhs, start=start, stop=stop),
                    reads=reads, writes=[out_t])

    def tr(out_t, out_ap, in_ap, ident, reads):
        return P.op("pe", lambda: nc.tensor.transpose(out_ap, in_ap, ident), reads=reads, writes=[out_t])

    def act(out_t, out_ap, in_ap, func, reads, bias=None, scale=None, eng="act"):
        def f():
            kw = {}
            if bias is not None:
                kw["bias"] = bias
            if scale is not None:
                kw["scale"] = scale
            return nc.scalar.activation(out_ap, in_ap, func, **kw)
        return P.op("act", f, reads=reads, writes=[out_t])

    def veng(e):
        return nc.vector if e == "dve" else nc.gpsimd

    def tt(e, out_t, out_ap, a, b, op, reads):
        return P.op(e, lambda: veng(e).tensor_tensor(out_ap, a, b, op), reads=reads, writes=[out_t])

    def ts(e, out_t, out_ap, a, s1, s2, op0, op1, reads):
        if op1 is None:
            return P.op(e, lambda: veng(e).tensor_scalar(out_ap, a, s1, None, op0), reads=reads, writes=[out_t])
        return P.op(e, lambda: veng(e).tensor_scalar(out_ap, a, s1, s2, op0, op1), reads=reads, writes=[out_t])

    def stt(e, out_t, out_ap, a, s, b, op0, op1, reads):
        w = out_t if isinstance(out_t, list) else [out_t]
        return P.op(e, lambda: veng(e).scalar_tensor_tensor(out_ap, a, s, b, op0, op1), reads=reads, writes=w)

    def cp(e, out_t, out_ap, in_ap, reads):
        if e == "act":
            return P.op("act", lambda: nc.scalar.copy(out_ap, in_ap), reads=reads, writes=[out_t])
        return P.op(e, lambda: veng(e).tensor_copy(out_ap, in_ap), reads=reads, writes=[out_t])

    s_par = P.new_dma_sem("par")
    P.op("sp", lambda: nc.sync.dma_start(out=par_sb[:, :], in_=par_d), writes=[par_t], dma_sem=s_par)
    s_cst = P.new_dma_sem("cst")
    P.op("sp", lambda: nc.sync.dma_start(out=cst_sb[:, :], in_=cst_d[:, 0:640]), writes=[cst_t], dma_sem=s_cst)
    cp("dve", cstb_t, cstb_sb[:, 0:128], identf, [cst_t])
    cp("dve", cstb_t, cstb_sb[:, 128:256], onesf, [cst_t])
    s_c2 = P.new_dma_sem("cst2")
    s_c3 = P.new_dma_sem("cst3")
    tmpa = f32s.get()
    tmpb = f32s.get()
    P.op("sp", lambda: nc.sync.dma_start(out=tmpa.ap[:, :], in_=cst_d[:, 640:1152]), writes=[tmpa], dma_sem=s_c2)
    P.op("sp", lambda: nc.sync.dma_start(out=tmpb.ap[:, :], in_=cst_d[:, 1152:1664]), writes=[tmpb], dma_sem=s_c3)
    cp("dve", cstb_t, cstb_sb[:, 256:768], tmpa.ap[:, :], [tmpa])
    cp("dve", cstb_t, cstb_sb[:, 768:1280], tmpb.ap[:, :], [tmpb])
    P.op("pool", lambda: nc.gpsimd.memset(misc_sb[:, 0:1], EPS), writes=[misc_t])
    P.op("pool", lambda: nc.gpsimd.memset(misc_sb[:, 1:2], -BIG), writes=[misc_t])
    P.op("pool", lambda: nc.gpsimd.memset(misc_sb[:, 2:3], 1.0), writes=[misc_t])
    P.op("pool", lambda: nc.gpsimd.memset(misc_sb[:, 3:4], float(np.log(128.0 ** -0.5))), writes=[misc_t])
    P.op("pool", lambda: nc.gpsimd.memset(misc_sb[:, 4:5], 0.0), writes=[misc_t])
    eps_ap = misc_sb[:, 0:1]
    nbig_ap = misc_sb[:, 1:2]
    one_ap = misc_sb[:, 2:3]
    lnq_ap = misc_sb[:, 3:4]
    for l in range(depth):
        P.op("pool", lambda l=l: nc.gpsimd.memset(S_sb[l][:, :], 0.0), writes=[S_t[l]])
        P.op("pool", lambda l=l: nc.gpsimd.memset(hist_sb[l][:, :], 0.0), writes=[hist_t[l]])
    for l in range(depth):
        a0 = poff(l) + OFF_ALOG
        act(negA_t, negA_sb[:, l * 4:(l + 1) * 4], par_sb[:, a0:a0 + 4], AF.Exp, [par_t])
    ts("dve", negA_t, negA_sb[:, :], negA_sb[:, :], -1.0, None, ALU.mult, None, [negA_t])

    conv_sem = {}
    conv_ops = {}
    s_wbd = P.new_dma_sem("wbd")
    for l in range(depth):
        P.op("pool", lambda l=l: nc.gpsimd.dma_start(
            out=wbd_sb[:, l * 64:(l + 1) * 64].rearrange("p (k n) -> p k n", k=KC),
            in_=w_in_d[l, :, 2048:2056].rearrange("(k p) n -> p k n", p=128)),
            writes=[wbd_t], dma_sem=s_wbd)

    def conv_dma(l, pi, col_off, src_ap, kch, ncols, pcols):
        t = wscr[l][pi]
        dst = t.ap.rearrange("p (k n) -> p k n", k=kch)[:, :, col_off:col_off + ncols]
        src = src_ap.rearrange("(k p) n -> p k n", p=128)
        key = (l, 0 if pi < 10 else (1 if pi < 21 else 2))
        if key not in conv_sem:
            conv_sem[key] = P.new_dma_sem("cv%d_%d" % key)
            conv_ops[key] = []
        o_ = P.op("pool", lambda: nc.gpsimd.dma_start(out=dst, in_=src), dma_sem=conv_sem[key])
        t.w = o_
        conv_ops[key].append(o_)

    for l in range(depth):
        for i_ in range(4):
            conv_dma(l, i_, 0, w_in_d[l, :, i_ * 512:(i_ + 1) * 512], KC, 512, 512)
        for g in range(4):
            for part in range(3):
                c0_ = 2056 + part * 512 + g * 128
                conv_dma(l, 4 + g, part * 128, w_in_d[l, :, c0_:c0_ + 128], KC, 128, 384)
        conv_dma(l, 8, 0, w_out_d[l, :, 0:512], KC, 512, 512)
        conv_dma(l, 9, 0, w_out_d[l, :, 512:1024], KC, 512, 512)
        for p in range(11):
            conv_dma(l, 10 + p, 0, wg_d[l, :, p * 256:(p + 1) * 256], KC, 256, 512)
            conv_dma(l, 10 + p, 256, wu_d[l, :, p * 256:(p + 1) * 256], KC, 256, 512)
        for hf in range(len(HFS)):
            r0_ = sum(HFS[:hf]) * 128
            for pn in range(2):
                conv_dma(l, 21 + hf * 2 + pn, 0, wd_d[l, r0_:r0_ + HFS[hf] * 128, pn * 512:(pn + 1) * 512],
                         HFS[hf], 512, 512)

    for key_, ops_ in conv_ops.items():
        fin_ = max(o_.semval for o_ in ops_)
        for o_ in ops_:
            o_.semval = fin_

    slot_ctr = [0, 0]

    def load_panel(l, pi):
        nm, kch, ncols, rg = specs[pi]
        si = slot_ctr[rg] % NSLOT
        slot_ctr[rg] += 1
        st = wslot[rg][si]
        src = wscr[l][pi]
        n = kch * ncols
        dst_sb = wslot_sb[rg][si]
        P.op("sp", lambda: nc.sync.dma_start(out=dst_sb[:, 0:n], in_=src.ap),
             reads=[src], writes=[st], dma_sem=wslot_sem[rg][si])
        view = dst_sb[:, 0:n].rearrange("p (k n) -> p k n", k=kch)
        return st, view

    def rms_stats(src_aps, src_ts, inv_d, eps_bias, pools, extra_bias=None):
        pp, fp_, bp = pools
        ps = pp.get()
        n = len(src_aps)
        for i in range(n):
            sq = bp.get()
            act(sq, sq.ap[:, :], src_aps[i], AF.Square, [src_ts[i]])
            mm(ps, ps.ap[:, :], onesb, sq.ap[:, :], i == 0, i == n - 1, [sq, cstb_t])
        rs = fp_.get()
        act(rs, rs.ap[:, :], ps.ap[:, :], AF.Ln, [ps, misc_t], bias=eps_bias, scale=inv_d)
        if extra_bias is None:
            act(rs, rs.ap[:, :], rs.ap[:, :], AF.Exp, [rs], scale=-0.5)
        else:
            act(rs, rs.ap[:, :], rs.ap[:, :], AF.Exp, [rs, misc_t], scale=-0.5, bias=extra_bias)
        return rs

    def rmsnorm_to_hT(goff, ln, pools):
        xres_sb, xres, hT_sb, hT = xresL_sb[ln], xresL[ln], hTL_sb[ln], hTL[ln]
        pp, fp_, bp = pools
        ps = pp.get()
        pend = None
        for i in range(KC + 1):
            cur = None
            if i < KC:
                sq = bp.get()
                act(sq, sq.ap[:, :], xres_sb[:, i, :], AF.Square, [xres[i]])
                cur = (i, sq)
            if pend is not None:
                pi_, psq = pend
                mm(ps, ps.ap[:, :], onesb, psq.ap[:, :], pi_ == 0, pi_ == KC - 1, [psq, cstb_t])
            pend = cur
            if i % 2 == 1:
                yield 0.3
        rs = fp_.get()
        act(rs, rs.ap[:, :], ps.ap[:, :], AF.Ln, [ps, misc_t], bias=eps_ap, scale=1.0 / D_MODEL)
        act(rs, rs.ap[:, :], rs.ap[:, :], AF.Exp, [rs], scale=-0.5)
        yield 0.3
        for k in range(KC):
            stt("dve", hT[k], hT_sb[:, k, :], xres_sb[:, k, :], par_sb[:, goff + k:goff + k + 1],
                rs.ap[:, :], ALU.mult, ALU.mult, [xres[k], rs, par_t])
        yield 0.5

    def proj_chunk(wt, wview, col0, rhs_list, rhs_ts, nk, pp):
        ps = pp.get()
        for k in range(nk):
            mm(ps, ps.ap[:, :], wview[:, k, col0:col0 + 128], rhs_list[k], k == 0, k == nk - 1,
               [wt, rhs_ts[k]])
        return ps


    def build_diag(l, d0, ntap):
        dt_ = diag_p.get()
        for j in range(ntap):
            col = poff(l) + OFF_DNC + d0 + j
            act(dt_, dt_.ap[:, j * 128:(j + 1) * 128], identb, AF.Copy, [cstb_t, par_t], scale=par_sb[:, col:col + 1])
        return dt_

    def mixer(l, ti, ln):
        xres_sb, xres, hT_sb, hT = xresL_sb[ln], xresL[ln], hTL_sb[ln], hTL[ln]
        hT_aps = [hT_sb[:, k, :] for k in range(KC)]
        psum = psum_m
        if l == 0:
            yield from load_x(ti, ln)
        yield from rmsnorm_to_hT(poff(l) + OFF_G1, ln, MP)
        delta_pre(l, ti, ln)
        yield 0.5
        hs = hist_sb[l]
        if l == 0 and ti == 0:
            dbg("xres0", xres, xres_sb[:, :, :])
            dbg("hT", hT, hT_sb[:, :, :])
        pendB = None
        pendC = None
        order = list(range(16))
        wt = wv = None
        for it in range(16 + 2):
            curB = None
            if it < 16:
                gc_ = order[it]
                if gc_ % 4 == 0:
                    wt, wv = load_panel(l, gc_ // 4)
                ps = proj_chunk(wt, wv, (gc_ % 4) * 128, hT_aps, hT, KC, psum)
                if gc_ < 12:
                    st = stg.get()
                    cp("pool", st, st.ap[:, 0:3], hs[:, gc_ * 3:gc_ * 3 + 3], [hist_t[l]])
                    cp("act", st, st.ap[:, 3:3 + NT], ps.ap[:, :], [ps])
                    cp("pool", hist_t[l], hs[:, gc_ * 3:gc_ * 3 + 3], st.ap[:, NT:NT + 3], [st])
                    dg = build_diag(l, gc_ * 4, 4)
                    curB = (gc_, st, dg)
                else:
                    act(sz[gc_ - 12], sz_sb[:, gc_ - 12, :], ps.ap[:, :], AF.Silu, [ps])
            curC = None
            if pendB is not None:
                pg, pst, pdg = pendB
                ps2 = psum.get()
                for j in range(4):
                    mm(ps2, ps2.ap[:, :], pdg.ap[:, j * 128:(j + 1) * 128], pst.ap[:, j:j + NT], j == 0, j == 3, [pdg, pst])
                act(qkv[pg], qkv_sb[:, pg, :], ps2.ap[:, :], AF.Silu, [ps2])
                if pg < 8:
                    sq = b16s.get()
                    act(sq, sq.ap[:, :], qkv_sb[:, pg, :], AF.Square, [qkv[pg]])
                    curC = (pg, sq)
            if pendC is not None:
                pg, psq = pendC
                ps3 = psum.get()
                mm(ps3, ps3.ap[:, :], onesb, psq.ap[:, :], True, True, [psq, cstb_t])
                rs = f32s.get()
                act(rs, rs.ap[:, :], ps3.ap[:, :], AF.Ln, [ps3, misc_t], bias=eps_ap, scale=1.0)
                if pg < 4:
                    act(rs, rs.ap[:, :], rs.ap[:, :], AF.Exp, [rs, misc_t], scale=-0.5, bias=lnq_ap)
                else:
                    act(rs, rs.ap[:, :], rs.ap[:, :], AF.Exp, [rs], scale=-0.5)
                tt("dve", qkv[pg], qkv_sb[:, pg, :], qkv_sb[:, pg, :], rs.ap[:, :], ALU.mult, [qkv[pg], rs])
            pendB, pendC = curB, curC
            yield 0.4
        if l == 0 and ti == 0:
            dbg("qkv", qkv, qkv_sb[:, :, :])
            dbg("sz", sz, sz_sb[:, :, :])
        yield from delta(l, ti, ln)
        def sc_A(g):
            wt, wv = load_panel(l, 4 + g)
            psB = proj_chunk(wt, wv, 0, hT_aps, hT, KC, psum)
            Bs = b16s.get()
            cp("act", Bs, Bs.ap[:, :], psB.ap[:, :], [psB])
            psC = proj_chunk(wt, wv, 128, hT_aps, hT, KC, psum)
            Cs = b16s.get()
            cp("act", Cs, Cs.ap[:, :], psC.ap[:, :], [psC])
            psH = proj_chunk(wt, wv, 256, hT_aps, hT, KC, psum)
            st = stg.get()
            h0 = 36 + g * 2
            cp("pool", st, st.ap[:, 0:2], hs[:, h0:h0 + 2], [hist_t[l]])
            tt("dve", st, st.ap[:, 2:2 + NT], psH.ap[:, :], Cs.ap[:, :], ALU.mult, [psH, Cs])
            cp("pool", hist_t[l], hs[:, h0:h0 + 2], st.ap[:, NT:NT + 2], [st])
            dg = build_diag(l, 48 + g * 3, 3)
            return {"g": g, "Bs": Bs, "st": st, "dg": dg}

        def sc_B(d):
            st, dg, Bs = d["st"], d["dg"], d["Bs"]
            psy = psum.get()
            for j in range(3):
                mm(psy, psy.ap[:, :], dg.ap[:, j * 128:(j + 1) * 128], st.ap[:, j:j + NT], j == 0, j == 2, [dg, st])
            ys = f32s.get()
            tt("dve", ys, ys.ap[:, :], psy.ap[:, :], Bs.ap[:, :], ALU.mult, [psy, Bs])
            sq = b16s.get()
            act(sq, sq.ap[:, :], ys.ap[:, :], AF.Square, [ys])
            d["ys"], d["sq"] = ys, sq

        def sc_C(d):
            g, ys, sq = d["g"], d["ys"], d["sq"]
            ps = psum.get()
            mm(ps, ps.ap[:, :], onesb, sq.ap[:, :], True, True, [sq, cstb_t])
            rs = f32s.get()
            act(rs, rs.ap[:, :], ps.ap[:, :], AF.Ln, [ps, misc_t], bias=eps_ap, scale=1.0 / 128.0)
            act(rs, rs.ap[:, :], rs.ap[:, :], AF.Exp, [rs], scale=-0.5)
            gcol = poff(l) + OFF_SCG + g
            stt("dve", ocat[4 + g], ocat_sb[:, 4 + g, :], ys.ap[:, :], par_sb[:, gcol:gcol + 1], rs.ap[:, :],
                ALU.mult, ALU.mult, [ys, rs, par_t])

        scd = {}
        for stg_i in range(6):
            if stg_i < 4:
                scd[stg_i] = sc_A(stg_i)
            if 0 <= stg_i - 1 < 4:
                sc_B(scd[stg_i - 1])
            if 0 <= stg_i - 2 < 4:
                sc_C(scd[stg_i - 2])
            yield 0.4
        if l == 0 and ti == 0:
            dbg("ocat", ocat, ocat_sb[:, 0:8, :])
        oc_aps = [ocat_sb[:, k, :] for k in range(8)]
        for oc in range(KC):
            if oc % 4 == 0:
                wt, wv = load_panel(l, 8 + oc // 4)
            ps = proj_chunk(wt, wv, (oc % 4) * 128, oc_aps, ocat, 8, psum)
            tt("dve", xres[oc], xres_sb[:, oc, :], ps.ap[:, :], xres_sb[:, oc, :], ALU.add, [ps, xres[oc]])
            yield 0.3

    def delta_pre(l, ti, ln):
        hT_sb, hT = hTL_sb[ln], hTL[ln]
        psum = psum_m
        psbd = psum.get()
        for ch in range(NCH):
            for k in range(KC):
                mm(psbd, psbd.ap[:, ch * 8:(ch + 1) * 8], hT_sb[:, k, ch * 128:(ch + 1) * 128],
                   wbd_sb[:, (l * KC + k) * 8:(l * KC + k + 1) * 8], k == 0, k == KC - 1, [hT[k], wbd_t])
        bd3 = psbd.ap[:, 0:NCH * 8].rearrange("p (c n) -> p c n", c=NCH)
        b_in = bd3[:, :, 0:4]
        a_in = bd3[:, :, 4:8]
        def v3(t):
            return t.ap[:, 0:NCH * 4].rearrange("p (c n) -> p c n", c=NCH)
        beta_t, g_t, eg_t, ed_t, egl_t, be_t, nbeta_t, tmp_t = tm
        act(tmp_t, v3(tmp_t), b_in, AF.Exp, [psbd], scale=-1.0)
        ts("dve", tmp_t, tmp_t.ap[:, :], tmp_t.ap[:, :], 1.0, None, ALU.add, None, [tmp_t])
        P.op("dve", lambda: nc.vector.reciprocal(beta_t.ap[:, :], tmp_t.ap[:, :]), reads=[tmp_t], writes=[beta_t])
        ts("dve", nbeta_t, nbeta_t.ap[:, :], beta_t.ap[:, :], -1.0, None, ALU.mult, None, [beta_t])
        y_t = smalls.get()
        dtb = par_sb[:, poff(l) + OFF_DTB:poff(l) + OFF_DTB + 4].unsqueeze(1).to_broadcast([128, NCH, 4])
        tt("dve", y_t, v3(y_t), a_in, dtb, ALU.add, [psbd, par_t])
        ay_t = smalls.get()
        ts("dve", ay_t, ay_t.ap[:, 0:16], y_t.ap[:, 0:16], -1.0, None, ALU.mult, None, [y_t])
        tt("dve", ay_t, ay_t.ap[:, 0:16], ay_t.ap[:, 0:16], y_t.ap[:, 0:16], ALU.max, [ay_t, y_t])
        act(ay_t, ay_t.ap[:, 0:16], ay_t.ap[:, 0:16], AF.Exp, [ay_t], scale=-1.0)
        act(ay_t, ay_t.ap[:, 0:16], ay_t.ap[:, 0:16], AF.Ln, [ay_t, misc_t], bias=one_ap, scale=1.0)
        ts("dve", y_t, y_t.ap[:, 0:16], y_t.ap[:, 0:16], 0.0, None, ALU.max, None, [y_t])
        tt("dve", y_t, y_t.ap[:, 0:16], y_t.ap[:, 0:16], ay_t.ap[:, 0:16], ALU.add, [y_t, ay_t])
        nA = negA_sb[:, l * 4:(l + 1) * 4].unsqueeze(1).to_broadcast([128, NCH, 4])
        tt("dve", g_t, v3(g_t), v3(y_t), nA, ALU.mult, [y_t, negA_t])
        psg = psum.get()
        mm(psg, psg.ap[:, 0:16], trif, g_t.ap[:, 0:16], True, True, [cst_t, g_t])
        mm(psg, psg.ap[:, 16:32], onesf, g_t.ap[:, 0:16], True, True, [cst_t, g_t])
        gcl = smalls.get()
        cp("dve", gcl, gcl.ap[:, 0:32], psg.ap[:, 0:32], [psg])
        tt("dve", gcl, gcl.ap[:, 32:48], gcl.ap[:, 16:32], gcl.ap[:, 0:16], ALU.subtract, [gcl])
        ex = smalls.get()
        act(ex, ex.ap[:, 0:48], gcl.ap[:, 0:48], AF.Exp, [gcl])
        cp("dve", eg_t, eg_t.ap[:, :], ex.ap[:, 0:16], [ex])
        cp("dve", egl_t, egl_t.ap[:, :], ex.ap[:, 16:32], [ex])
        cp("dve", ed_t, ed_t.ap[:, :], ex.ap[:, 32:48], [ex])
        tt("dve", be_t, be_t.ap[:, :], beta_t.ap[:, :], eg_t.ap[:, :], ALU.mult, [beta_t, eg_t])

    def delta(l, ti, ln):
        hT_sb, hT = hTL_sb[ln], hTL[ln]
        psum = psum_m
        beta_t, g_t, eg_t, ed_t, egl_t, be_t, nbeta_t, tmp_t = tm

        def bc(t, ch):
            return t.ap[:, ch * 4:(ch + 1) * 4].unsqueeze(2).to_broadcast([128, 4, 128])

        def h3(ap):
            return ap.rearrange("p (h c) -> p h c", h=4)

        if l == 0 and ti == 0:
            dbg("beta", [beta_t], beta_t.ap[:, :])
            dbg("g", [g_t], g_t.ap[:, :])
            dbg("eg", [eg_t], eg_t.ap[:, :])
            dbg("ed", [ed_t], ed_t.ap[:, :])
            dbg("egl", [egl_t], egl_t.ap[:, :])
        def hmm(ps_t, lhs_t, rhs_t):
            for h in range(H):
                hs_ = slice(h * 128, (h + 1) * 128)
                mm(ps_t, ps_t.ap[:, hs_], lhs_t.ap[:, hs_], rhs_t.ap[:, hs_], True, True, [lhs_t, rhs_t])

        def s_front(c):
            ch = c["ch"]
            cs = c["cs"]
            gts = f32s.get()
            for h in range(H):
                stt("dve", gts, gts.ap[:, h * 128:(h + 1) * 128], trif, g_t.ap[:, ch * 4 + h:ch * 4 + h + 1], bigif,
                    ALU.mult, ALU.add, [cst_t, g_t])
            ptk = psum.get()
            ptkb = ptk.ap[:, :].bitcast(BF16)[:, 0:512]
            for h in range(H):
                tr(ptk, ptkb[:, h * 128:(h + 1) * 128], qkv_sb[:, 4 + h, cs], identb, [qkv[4 + h], cstb_t])
            ptv = psum.get()
            ptvb = ptv.ap[:, :].bitcast(BF16)[:, 0:512]
            for h in range(H):
                tr(ptv, ptvb[:, h * 128:(h + 1) * 128], qkv_sb[:, 8 + h, cs], identb, [qkv[8 + h], cstb_t])
            Ru = r_p.get()
            Rw = r_p.get()
            kdec = kdec_p.get()
            tt("dve", Ru, h3(Ru.ap[:, :]), h3(ptvb), bc(beta_t, ch), ALU.mult, [ptv, beta_t])
            tt("dve", Rw, h3(Rw.ap[:, :]), h3(ptkb), bc(be_t, ch), ALU.mult, [ptk, be_t])
            tt("dve", kdec, h3(kdec.ap[:, :]), h3(ptkb), bc(ed_t, ch), ALU.mult, [ptk, ed_t])
            c["Ru"], c["Rw"], c["kdec"] = Ru, Rw, kdec
            c["gts"] = gts

        def s_front2(c):
            ch = c["ch"]
            cs = c["cs"]
            gts = c["gts"]
            psD = psum.get()
            for h in range(H):
                mm(psD, psD.ap[:, h * 128:(h + 1) * 128], gts.ap[:, h * 128:(h + 1) * 128], smf, True, True,
                   [gts, cst_t])
            psG = psum.get()
            mm(psG, psG.ap[:, :], onesf, gts.ap[:, :], True, True, [gts, cst_t])
            decS = dec_p.get()
            act(decS, decS.ap[:, :], psD.ap[:, :], AF.Exp, [psD, misc_t], bias=nbig_ap, scale=1.0)
            egrow = dec_p.get()
            act(egrow, egrow.ap[:, :], psG.ap[:, :], AF.Exp, [psG, misc_t], bias=nbig_ap, scale=1.0)
            decI = dec_p.get()
            tt("pool", decI, decI.ap[:, :], decS.ap[:, :], i4b, ALU.add, [decS, cstb_t])
            psKK = psum.get()
            for h in range(H):
                mm(psKK, psKK.ap[:, h * 128:(h + 1) * 128], qkv_sb[:, 4 + h, cs], qkv_sb[:, 4 + h, cs], True, True,
                   [qkv[4 + h]])
            B0 = b16s.get()
            for h in range(H):
                hs_ = slice(h * 128, (h + 1) * 128)
                stt("dve", B0, B0.ap[:, hs_], psKK.ap[:, hs_], nbeta_t.ap[:, ch * 4 + h:ch * 4 + h + 1],
                    decS.ap[:, hs_], ALU.mult, ALU.mult, [psKK, nbeta_t, decS])
            Bk = ab_p.get()
            tt("pool", Bk, Bk.ap[:, :], B0.ap[:, :], bd4b, ALU.mult, [B0, cstb_t])
            Boff = boff_p.get()
            tt("pool", Boff, Boff.ap[:, :], B0.ap[:, :], Bk.ap[:, :], ALU.subtract, [B0, Bk])
            psQK = psum.get()
            for h in range(H):
                mm(psQK, psQK.ap[:, h * 128:(h + 1) * 128], qkv_sb[:, h, cs], qkv_sb[:, 4 + h, cs], True, True,
                   [qkv[h], qkv[4 + h]])
            qk = b16s.get()
            tt("dve", qk, qk.ap[:, :], psQK.ap[:, :], decI.ap[:, :], ALU.mult, [psQK, decI])
            qdec = qdec_p.get()
            tt("pool", qdec, h3(qdec.ap[:, :]), qkv_sb[:, 0:4, cs], h3(egrow.ap[:, :]), ALU.mult,
               [qkv[0], qkv[1], qkv[2], qkv[3], egrow])
            c["Bk"], c["Boff"], c["qk"], c["qdec"] = Bk, Boff, qk, qdec

        def s_tr(c):
            Bk, qk = c["Bk"], c["qk"]
            pta = psum.get()
            ptab = pta.ap[:, :].bitcast(BF16)
            for h in range(H):
                tr(pta, ptab[:, h * 128:(h + 1) * 128], Bk.ap[:, h * 128:(h + 1) * 128], identb, [Bk, cstb_t])
            for h in range(H):
                tr(pta, ptab[:, 512 + h * 128:512 + (h + 1) * 128], qk.ap[:, h * 128:(h + 1) * 128], identb,
                   [qk, cstb_t])
            Ak = ab_p.get()
            qkT = qkT_p.get()
            cp("act", Ak, Ak.ap[:, :], ptab[:, 0:512], [pta])
            cp("act", qkT, qkT.ap[:, :], ptab[:, 512:1024], [pta])
            Pk = chain_p.get()
            Qk = chain_p.get()
            tt("pool", Pk, Pk.ap[:, :], Ak.ap[:, :], i4b, ALU.add, [Ak, cstb_t])
            tt("pool", Qk, Qk.ap[:, :], Bk.ap[:, :], i4b, ALU.add, [Bk, cstb_t])
            c["Ak"], c["qkT"], c["Pk"], c["Qk"] = Ak, qkT, Pk, Qk

        def s_h1(c, lev):
            Ak, Bk = c["Ak"], c["Bk"]
            psA = psum.get()
            hmm(psA, Bk, Ak)
            A2 = ab_p.get()
            cp("act", A2, A2.ap[:, :], psA.ap[:, :], [psA])
            c["A2"] = A2
            if lev < 4:
                psB = psum.get()
                hmm(psB, Ak, Bk)
                B2 = ab_p.get()
                cp("act", B2, B2.ap[:, :], psB.ap[:, :], [psB])
                c["B2"] = B2

        def s_h2(c, lev):
            Pk, Qk, A2 = c["Pk"], c["Qk"], c["A2"]
            psP = psum.get()
            hmm(psP, Qk, A2)
            P2 = chain_p.get()
            tt("dve", P2, P2.ap[:, :], psP.ap[:, :], Pk.ap[:, :], ALU.add, [psP, Pk])
            psQ = psum.get()
            hmm(psQ, A2, Qk)
            Q2 = chain_p.get()
            tt("dve", Q2, Q2.ap[:, :], psQ.ap[:, :], Qk.ap[:, :], ALU.add, [psQ, Qk])
            c["Pk"], c["Qk"], c["Ak"] = P2, Q2, A2
            if lev < 4:
                c["Bk"] = c["B2"]

        def s_fin1(c):
            psGm = psum.get()
            hmm(psGm, c["Boff"], c["Pk"])
            Gm = b16s.get()
            cp("act", Gm, Gm.ap[:, :], psGm.ap[:, :], [psGm])
            c["Gm"] = Gm

        def s_fin2(c):
            Pk = c["Pk"]
            psT = psum.get()
            hmm(psT, c["Qk"], c["Gm"])
            TT = b16s.get()
            tt("dve", TT, TT.ap[:, :], psT.ap[:, :], Pk.ap[:, :], ALU.add, [psT, Pk])
            c["TT"] = TT

        def s_fin3(c):
            TT = c["TT"]
            psu = psum.get()
            hmm(psu, TT, c["Ru"])
            u = f32s.get()
            cp("act", u, u.ap[:, :], psu.ap[:, :], [psu])
            psw = psum.get()
            hmm(psw, c["Rw"], TT)
            wT = wT_p.get()
            cp("act", wT, wT.ap[:, :], psw.ap[:, :], [psw])
            c["u"], c["wT"] = u, wT

        def s_rec1(c):
            u, wT = c["u"], c["wT"]
            Sb = Sb_sb[l]
            psws = psum.get()
            for h in range(H):
                hs_ = slice(h * 128, (h + 1) * 128)
                mm(psws, psws.ap[:, hs_], wT.ap[:, hs_], Sb[:, hs_], True, True, [wT, Sb_t[l]])
            vnew = b16s.get()
            tt("dve", vnew, vnew.ap[:, :], u.ap[:, :], psws.ap[:, :], ALU.subtract, [u, psws])
            c["vnew"] = vnew

        def s_rec2(c):
            ch, cs = c["ch"], c["cs"]
            qdec, qkT, kdec, vnew = c["qdec"], c["qkT"], c["kdec"], c["vnew"]
            Sb = Sb_sb[l]
            pso = psum.get()
            for h in range(H):
                hs_ = slice(h * 128, (h + 1) * 128)
                mm(pso, pso.ap[:, hs_], Sb[:, hs_], qdec.ap[:, hs_], True, False, [Sb_t[l], qdec])
                mm(pso, pso.ap[:, hs_], vnew.ap[:, hs_], qkT.ap[:, hs_], False, True, [vnew, qkT])
            psS = psum.get()
            for h in range(H):
                hs_ = slice(h * 128, (h + 1) * 128)
                mm(psS, psS.ap[:, hs_], kdec.ap[:, hs_], vnew.ap[:, hs_], True, True, [kdec, vnew])
            for h in range(H):
                hs_ = slice(h * 128, (h + 1) * 128)
                stt("dve", S_t[l], S_sb[l][:, hs_], S_sb[l][:, hs_], egl_t.ap[:, ch * 4 + h:ch * 4 + h + 1],
                    psS.ap[:, hs_], ALU.mult, ALU.add, [S_t[l], egl_t, psS])
            cp("act", Sb_t[l], Sb[:, :], S_sb[l][:, :], [S_t[l]])
            sq = b16s.get()
            act(sq, sq.ap[:, :], pso.ap[:, :], AF.Square, [pso])
            c["pso"], c["sq"] = pso, sq

        def s_rec3(c):
            cs, pso, sq = c["cs"], c["pso"], c["sq"]
            ps = psum.get()
            mm(ps, ps.ap[:, :], onesb, sq.ap[:, :], True, True, [sq, cstb_t])
            rs = f32s.get()
            act(rs, rs.ap[:, :], ps.ap[:, :], AF.Ln, [ps, misc_t], bias=eps_ap, scale=1.0 / 128.0)
            act(rs, rs.ap[:, :], rs.ap[:, :], AF.Exp, [rs], scale=-0.5)
            on = f32s.get()
            tt("dve", on, on.ap[:, :], pso.ap[:, :], rs.ap[:, :], ALU.mult, [pso, rs])
            gcol = poff(l) + OFF_DNG
            stt("dve", [ocat[0], ocat[1], ocat[2], ocat[3]], ocat_sb[:, 0:4, cs], h3(on.ap[:, :]),
                par_sb[:, gcol:gcol + 1], sz_sb[:, 0:4, cs],
                ALU.mult, ALU.mult, [on, par_t, sz[0], sz[1], sz[2], sz[3]])

        cp("act", Sb_t[l], Sb_sb[l][:, :], S_sb[l][:, :], [S_t[l]])
        yield 1.0
        for pr in range(NCH // 2):
            cc = [{"ch": pr * 2 + i, "cs": slice((pr * 2 + i) * 128, (pr * 2 + i + 1) * 128)} for i in range(2)]
            for c in cc:
                s_front(c)
            yield 1.0
            for c in cc:
                s_front2(c)
            yield 1.0
            yield 1.0
            for c in cc:
                s_tr(c)
            yield 1.0
            for lev in range(6):
                for c in cc:
                    if lev >= 1:
                        s_h2(c, lev - 1)
                    if lev < 5:
                        s_h1(c, lev)
                yield 1.0
            for c in cc:
                s_fin1(c)
            yield 1.0
            for c in cc:
                s_fin2(c)
            yield 1.0
            for c in cc:
                s_fin3(c)
            yield 1.0
            s_rec1(cc[0])
            yield 1.0
            s_rec2(cc[0])
            yield 1.0
            s_rec1(cc[1])
            s_rec3(cc[0])
            yield 1.0
            s_rec2(cc[1])
            yield 1.0
            s_rec3(cc[1])
            yield 1.0

    def ffn(l, ti, ln):
        xres_sb, xres, hT_sb, hT = xresL_sb[ln], xresL[ln], hTL_sb[ln], hTL[ln]
        hT_aps = [hT_sb[:, k, :] for k in range(KC)]
        psum = psum_f
        if l == 0 and ti == 0:
            dbg("xres1", xres, xres_sb[:, :, :])
        yield from rmsnorm_to_hT(poff(l) + OFF_G2, ln, FP)
        act_aps = [act_sb[:, k, :] for k in range(HF)]

        def down_half(hf):
            for pn in range(2):
                wt, wv = load_panel(l, 21 + hf * 2 + pn)
                for c in range(4):
                    oc = pn * 4 + c
                    ps = proj_chunk(wt, wv, c * 128, act_aps, actT, HFS[hf], psum)
                    tt("dve", xres[oc], xres_sb[:, oc, :], ps.ap[:, :], xres_sb[:, oc, :], ALU.add, [ps, xres[oc]])
                    yield

        for p in range(11):
            wt, wv = load_panel(l, 10 + p)
            for c in range(2):
                fc = p * 2 + c
                psg_ = proj_chunk(wt, wv, c * 128, hT_aps, hT, KC, psum)
                psu_ = proj_chunk(wt, wv, 256 + c * 128, hT_aps, hT, KC, psum)
                sg = b16f.get()
                act(sg, sg.ap[:, :], psg_.ap[:, :], AF.Silu, [psg_])
                tt("dve", actT[fc % HF], act_sb[:, fc % HF, :], psu_.ap[:, :], sg.ap[:, :], ALU.mult, [psu_, sg])
                yield
            if p == 3:
                yield from down_half(0)
            elif p == 7:
                yield from down_half(1)
        yield from down_half(2)
        if l == depth - 1:
            yield from store_out(ti, ln)

    def load_x(ti, ln):
        xres_sb, xres = xresL_sb[ln], xresL[ln]
        psum = psum_m
        for blk in range(NT // 128):
            r0 = ti * NT + blk * 128
            for half in range(2):
                i = xin.i % NXI
                xt = xin.get()
                P.op("pool", lambda xt=xt, r0=r0, half=half: nc.gpsimd.dma_start(
                    out=xt.ap[:, :], in_=x_d[r0:r0 + 128, half * 512:(half + 1) * 512]),
                    writes=[xt], dma_sem=xin_sem[i])
                ps = psum.get()
                for kk in range(4):
                    k = half * 4 + kk
                    tr(ps, ps.ap[:, kk * 128:(kk + 1) * 128], xt.ap[:, kk * 128:(kk + 1) * 128], identf, [xt, cst_t])
                for kk in range(4):
                    k = half * 4 + kk
                    cp("dve", xres[k], xres_sb[:, k, blk * 128:(blk + 1) * 128], ps.ap[:, kk * 128:(kk + 1) * 128], [ps])
            yield 0.3

    def store_out(ti, ln):
        xres_sb, xres = xresL_sb[ln], xresL[ln]
        psum = psum_f
        rs = rms_stats([xres_sb[:, k, :] for k in range(KC)], xres, 1.0 / D_MODEL, eps_ap, FP)
        for k in range(KC):
            stt("dve", xres[k], xres_sb[:, k, :], xres_sb[:, k, :], par_sb[:, OFF_GF + k:OFF_GF + k + 1],
                rs.ap[:, :], ALU.mult, ALU.mult, [xres[k], rs, par_t])
        yield
        for blk in range(NT // 128):
            r0 = ti * NT + blk * 128
            for half in range(2):
                i = xout.i % NXB
                xo = xout.get()
                ps = psum.get()
                for kk in range(4):
                    k = half * 4 + kk
                    tr(ps, ps.ap[:, kk * 128:(kk + 1) * 128], xres_sb[:, k, blk * 128:(blk + 1) * 128], identf,
                       [xres[k], cst_t])
                cp("act", xo, xo.ap[:, :], ps.ap[:, :], [ps])
                P.op("sp", lambda xo=xo, r0=r0, half=half: nc.sync.dma_start(
                    out=out_d[r0:r0 + 128, half * 512:(half + 1) * 512], in_=xo.ap[:, :]),
                    reads=[xo], dma_sem=xout_sem[i])
            yield

    EST = {"M": 60.0, "F": 50.0}
    lane_phases = [[], []]
    for ti in range(ntiles):
        ln = ti % 2
        for l in range(depth):
            lane_phases[ln].append(("M", mixer(l, ti, ln)))
            lane_phases[ln].append(("F", ffn(l, ti, ln)))
    lane_phases[1] = [None] + lane_phases[1]
    nslots = max(len(lane_phases[0]), len(lane_phases[1]))
    for s_ in range(nslots):
        cur = []
        for ln in range(2):
            if s_ < len(lane_phases[ln]) and lane_phases[ln][s_] is not None:
                kind, gen = lane_phases[ln][s_]
                cur.append([gen, EST[kind], 0])
        while cur:
            cur.sort(key=lambda e: (e[2] + 1.0) / e[1])
            e = cur[0]
            try:
                w_ = next(e[0])
                e[2] += (1.0 if w_ is None else w_)
            except StopIteration:
                cur.remove(e)

    print("sbuf bytes remaining/partition:", nc.sbuf_bytes_remaining, flush=True)
    P.emit()
    return nc


def _layout_params(inp, depth):
    cols = []
    for l in range(depth):
        cols.append(inp["norm1_g"][l].reshape(8, 128).T)
        cols.append(inp["norm2_g"][l].reshape(8, 128).T)
        cols.append(inp["dn_conv_w"][l].reshape(4, 12, 128).transpose(2, 1, 0).reshape(128, 48))
        cols.append(inp["sc_conv_w"][l].reshape(3, 4, 128).transpose(2, 1, 0).reshape(128, 12))
        cols.append(inp["dn_norm_g"][l].reshape(128, 1))
        cols.append(inp["sc_norm_g"][l].reshape(4, 128).T)
        cols.append(np.broadcast_to(inp["dn_a_log"][l][None, :], (128, 4)))
        cols.append(np.broadcast_to(inp["dn_dt_bias"][l][None, :], (128, 4)))
    cols.append(inp["final_norm_g"].reshape(8, 128).T)
    return np.ascontiguousarray(np.concatenate(cols, axis=1).astype(np.float32))


def _consts():
    j = np.arange(128)
    ident = np.eye(128, dtype=np.float32)
    tri = (j[:, None] <= j[None, :]).astype(np.float32)
    sm = (j[:, None] > j[None, :]).astype(np.float32)
    bigi = ident * BIG
    ones = np.ones((128, 128), np.float32)
    i4 = np.tile(ident, (1, 4))
    bd = ((j[:, None] // 64) == (j[None, :] // 64)).astype(np.float32)
    bd4 = np.tile(bd, (1, 4))
    return np.ascontiguousarray(np.concatenate([ident, tri, sm, bigi, ones, i4, bd4], axis=1))


_NC_CACHE = {}
_LAST = {}


def run(inputs, seq, depth, ncores, debug_out=None):
    key = (seq, depth)
    if key not in _NC_CACHE:
        _NC_CACHE[key] = build_program(seq, depth, debug_out)
    nc = _NC_CACHE[key]
    params = _layout_params(inputs, depth)
    consts = _consts()
    in_maps = []
    for c in range(ncores):
        in_maps.append({
            "x": np.ascontiguousarray(inputs["x"][c, :seq]),
            "w_in": np.ascontiguousarray(inputs["w_in"][:depth]),
            "w_out": np.ascontiguousarray(inputs["w_out"][:depth]),
            "ffn_w_gate": np.ascontiguousarray(inputs["ffn_w_gate"][:depth]),
            "ffn_w_up": np.ascontiguousarray(inputs["ffn_w_up"][:depth]),
            "ffn_w_down": np.ascontiguousarray(inputs["ffn_w_down"][:depth]),
            "params": params,
            "consts": consts,
        })
    res = run_bass_kernel_spmd(nc, in_maps, core_ids=list(range(ncores)))
    _LAST["res"] = res.results
    return np.stack([res.results[c]["out"] for c in range(ncores)], axis=0)


def kernel(**inputs):
    inputs = {k: np.asarray(v) for k, v in inputs.items()}
    out = run(inputs, SEQ, DEPTH, BATCH)
    return out.astype(np.float32)
```

```python
import numpy as np
import concourse.bass as bass
import concourse.mybir as mybir
from concourse.bass_utils import run_bass_kernel_spmd

F32 = mybir.dt.float32
BF16 = mybir.dt.bfloat16
AF = mybir.ActivationFunctionType
ALU = mybir.AluOpType

D_MODEL = 1024
KC = 8
DEPTH = 4
SEQ = 8192
BATCH = 8
H = 4
DFF = 2816
FC = 22
W_IN_COLS = 3592
NT = 512
NCH = NT // 128
BIG = 128.0
EPS = 1e-6
NSLOT = 3
HFS = (8, 8, 6)
SLOT_ELEMS = 4096

ENGS = ("pe", "act", "dve", "pool", "sp")
TAG_OPS = False


class T:
    __slots__ = ("ap", "w", "r", "name")

    def __init__(self, ap, name=""):
        self.ap = ap
        self.w = None
        self.r = []
        self.name = name


class Op:
    __slots__ = ("eng", "fn", "deps", "is_dma", "sem", "semval", "needs_inc", "incval", "tag")

    def __init__(self, eng, fn, deps, is_dma=False):
        self.eng = eng
        self.fn = fn
        self.deps = deps
        self.is_dma = is_dma
        self.sem = None
        self.semval = 0
        self.needs_inc = False
        self.incval = 0


class Prog:
    def __init__(self, nc):
        self.nc = nc
        self.ops = {e: [] for e in ENGS}
        self.order = []
        self.dma_sems = []

    def new_dma_sem(self, name):
        h = self.nc.alloc_semaphore(name)
        self.dma_sems.append([h, 0])
        return len(self.dma_sems) - 1

    def op(self, eng, fn, reads=(), writes=(), dma_sem=None):
        deps = []
        for t in reads:
            if t.w is not None:
                deps.append(t.w)
        for t in writes:
            if t.w is not None:
                deps.append(t.w)
            deps.extend(t.r)
        o = Op(eng, fn, deps, is_dma=dma_sem is not None)
        o.tag = None
        if TAG_OPS:
            import sys as _sys
            f_ = _sys._getframe(1)
            names = []
            while f_ is not None and len(names) < 6:
                nm_ = f_.f_code.co_name
                if nm_ not in ("mm", "hmm", "tt", "ts", "stt", "cp", "act", "tr", "op", "proj_chunk", "<lambda>"):
                    names.append(nm_)
                f_ = f_.f_back
            o.tag = "/".join(names[:2])
        if dma_sem is not None:
            ent = self.dma_sems[dma_sem]
            ent[1] += 16
            o.sem = ent[0]
            o.semval = ent[1]
        for t in reads:
            t.r.append(o)
        for t in writes:
            t.w = o
            t.r = []
        self.ops[eng].append(o)
        self.order.append(o)
        return o

    def emit(self):
        nc = self.nc
        for o in self.order:
            for d in o.deps:
                if d.is_dma:
                    continue
                if d.eng != o.eng or o.eng != "pe":
                    d.needs_inc = True
        prog_sem = {e: nc.alloc_semaphore("prog_" + e) for e in ENGS}
        for e in ENGS:
            c = 0
            for o in self.ops[e]:
                if o.needs_inc and not o.is_dma:
                    c += 1
                    o.incval = c
        engobj = {"pe": nc.tensor, "act": nc.scalar, "dve": nc.vector, "pool": nc.gpsimd, "sp": nc.sync}

        def run_engine(e):
            eng = engobj[e]
            waited = {}
            for o in self.ops[e]:
                need = {}
                for d in o.deps:
                    if d.is_dma:
                        key = ("d", id(d.sem))
                        if need.get(key, (None, 0))[1] < d.semval:
                            need[key] = (d.sem, d.semval)
                    elif d.eng != e or e != "pe":
                        key = ("e", d.eng)
                        if need.get(key, (None, 0))[1] < d.incval:
                            need[key] = (prog_sem[d.eng], d.incval)
                for key, (sem, val) in need.items():
                    if waited.get(key, 0) < val:
                        eng.wait_ge(sem, val)
                        waited[key] = val
                ins = o.fn()
                if o.tag is not None:
                    ins.annotate(o.tag)
                if o.is_dma:
                    ins.then_inc(o.sem, 16)
                elif o.needs_inc:
                    ins.then_inc(prog_sem[e], 1)
            return waited

        with nc.Block() as block:
            @block.tensor
            def _(t):
                run_engine("pe")

            @block.scalar
            def _(a):
                run_engine("act")

            @block.vector
            def _(v):
                run_engine("dve")

            @block.gpsimd
            def _(g):
                run_engine("pool")

            @block.sync
            def _(s):
                w = run_engine("sp")
                for h, cnt in self.dma_sems:
                    if cnt > 0 and w.get(("d", id(h)), 0) < cnt:
                        nc.sync.wait_ge(h, cnt)


class Pool_:
    def __init__(self, tiles):
        self.tiles = tiles
        self.i = 0

    def get(self):
        t = self.tiles[self.i % len(self.tiles)]
        self.i += 1
        return t


def panel_specs():
    specs = []
    for nm in ("q", "k", "v", "z"):
        specs.append(("in_" + nm, KC, 512, 0))
    for g in range(4):
        specs.append(("in_sc%d" % g, KC, 384, 0))
    specs.append(("out0", KC, 512, 0))
    specs.append(("out1", KC, 512, 0))
    for p in range(11):
        specs.append(("gu%d" % p, KC, 512, 1))
    for hf in range(len(HFS)):
        for pn in range(2):
            specs.append(("dn%d_%d" % (hf, pn), HFS[hf], 512, 1))
    return specs


def build_program(seq=SEQ, depth=DEPTH, debug_out=None):
    ntiles = seq // NT
    nc = bass.Bass("TRN2", target_bir_lowering=False)
    P = Prog(nc)

    x_d = nc.dram_tensor("x", [seq, D_MODEL], F32, kind="ExternalInput").ap()
    out_d = nc.dram_tensor("out", [seq, D_MODEL], F32, kind="ExternalOutput").ap()
    w_in_d = nc.dram_tensor("w_in", [depth, D_MODEL, W_IN_COLS], F32, kind="ExternalInput").ap()
    w_out_d = nc.dram_tensor("w_out", [depth, D_MODEL, D_MODEL], F32, kind="ExternalInput").ap()
    wg_d = nc.dram_tensor("ffn_w_gate", [depth, D_MODEL, DFF], F32, kind="ExternalInput").ap()
    wu_d = nc.dram_tensor("ffn_w_up", [depth, D_MODEL, DFF], F32, kind="ExternalInput").ap()
    wd_d = nc.dram_tensor("ffn_w_down", [depth, DFF, D_MODEL], F32, kind="ExternalInput").ap()
    NPAR = depth * (8 + 8 + 48 + 12 + 1 + 4 + 4 + 4) + 8
    par_d = nc.dram_tensor("params", [128, NPAR], F32, kind="ExternalInput").ap()
    NCONST = 128 * 5 + 512 + 512
    cst_d = nc.dram_tensor("consts", [128, NCONST], F32, kind="ExternalInput").ap()

    specs = panel_specs()
    wscr = []
    for l in range(depth):
        row = []
        for (nm, kch, ncols, _rg) in specs:
            d = nc.dram_tensor("ws_%d_%s" % (l, nm), [128, kch * ncols], BF16, kind="Internal").ap()
            row.append(T(d, "ws_%d_%s" % (l, nm)))
        wscr.append(row)

    def sb(name, shape, dt):
        return nc.alloc_sbuf_tensor(name, shape, dt)

    par_sb = sb("par", [128, NPAR], F32)
    cst_sb = sb("cst", [128, 640], F32)
    par_t = T(par_sb, "par")
    cst_t = T(cst_sb, "cst")
    identf = cst_sb[:, 0:128]
    trif = cst_sb[:, 128:256]
    smf = cst_sb[:, 256:384]
    bigif = cst_sb[:, 384:512]
    onesf = cst_sb[:, 512:640]
    cstb_sb = sb("cstb", [128, 256 + 512 + 512], BF16)
    cstb_t = T(cstb_sb, "cstb")
    identb = cstb_sb[:, 0:128]
    onesb = cstb_sb[:, 128:256]
    i4b = cstb_sb[:, 256:768]
    bd4b = cstb_sb[:, 768:1280]
    misc_sb = sb("misc", [128, 8], F32)
    misc_t = T(misc_sb, "misc")

    def poff(l):
        return l * 89
    OFF_G1, OFF_G2, OFF_DNC, OFF_SCC, OFF_DNG, OFF_SCG, OFF_ALOG, OFF_DTB = 0, 8, 16, 64, 76, 77, 81, 85
    OFF_GF = depth * 89

    NL = 2
    xresL_sb = [sb("xres_%d" % i, [128, KC, NT], F32) for i in range(NL)]
    xresL = [[T(xresL_sb[i][:, k, :], "xres%d_%d" % (i, k)) for k in range(KC)] for i in range(NL)]
    hTL_sb = [sb("hT_%d" % i, [128, KC, NT], BF16) for i in range(NL)]
    hTL = [[T(hTL_sb[i][:, k, :], "hT%d_%d" % (i, k)) for k in range(KC)] for i in range(NL)]
    qkv_sb = sb("qkvT", [128, 12, NT], BF16)
    qkv = [T(qkv_sb[:, k, :], "qkv%d" % k) for k in range(12)]
    sz_sb = sb("szT", [128, 4, NT], BF16)
    sz = [T(sz_sb[:, k, :], "sz%d" % k) for k in range(4)]
    ocat_sb = qkv_sb
    ocat = qkv[0:8]
    HF = HFS[0]
    act_sb = sb("actT", [128, HF, NT], BF16)
    actT = [T(act_sb[:, k, :], "act%d" % k) for k in range(HF)]
    wslot_sb = [[sb("wslot%d_%d" % (r, i), [128, SLOT_ELEMS], BF16) for i in range(NSLOT)] for r in range(2)]
    wslot = [[T(wslot_sb[r][i], "wslot%d_%d" % (r, i)) for i in range(NSLOT)] for r in range(2)]
    wslot_sem = [[P.new_dma_sem("wsl%d_%d" % (r, i)) for i in range(NSLOT)] for r in range(2)]
    wbd_sb = sb("wbd", [128, depth * KC * 8], BF16)
    wbd_t = T(wbd_sb, "wbd")
    S_sb = [sb("S%d" % l, [128, 512], F32) for l in range(depth)]
    S_t = [T(S_sb[l], "S%d" % l) for l in range(depth)]
    Sb_one = sb("Sb", [128, 512], BF16)
    Sb_one_t = T(Sb_one, "Sb")
    Sb_sb = [Sb_one for l in range(depth)]
    Sb_t = [Sb_one_t for l in range(depth)]
    hist_sb = [sb("hist%d" % l, [128, 12 * 3 + 4 * 2], BF16) for l in range(depth)]
    hist_t = [T(hist_sb[l], "hist%d" % l) for l in range(depth)]
    negA_sb = sb("negA", [128, depth * 4], F32)
    negA_t = T(negA_sb, "negA")
    NXB = 1
    NXI = 2
    xin_sb = [sb("xin%d" % i, [128, 512], F32) for i in range(NXI)]
    xin = Pool_([T(xin_sb[i], "xin%d" % i) for i in range(NXI)])
    xin_sem = [P.new_dma_sem("xin%d" % i) for i in range(NXI)]
    xout_sb = [sb("xout%d" % i, [128, 512], F32) for i in range(NXB)]
    xout = Pool_([T(xout_sb[i], "xout%d" % i) for i in range(NXB)])
    xout_sem = [P.new_dma_sem("xout%d" % i) for i in range(NXB)]
    NF = 5
    f32s_sb = [sb("f32s%d" % i, [128, 512], F32) for i in range(NF)]
    f32s = Pool_([T(f32s_sb[i], "f32s%d" % i) for i in range(NF)])
    f32f_sb = [sb("f32f%d" % i, [128, 512], F32) for i in range(1)]
    f32f = Pool_([T(f32f_sb[i], "f32f%d" % i) for i in range(1)])
    def bpool(name, n):
        hs = [sb("%s%d" % (name, i), [128, 512], BF16) for i in range(n)]
        return Pool_([T(hs[i], "%s%d" % (name, i)) for i in range(n)])
    b16s = bpool("b16s", 6)
    b16f = bpool("b16f", 2)
    kdec_p = bpool("kdec", 2)
    qkT_p = bpool("qkT", 2)
    qdec_p = bpool("qdec", 2)
    chain_p = bpool("chain", 8)
    dec_p = bpool("dec", 6)
    wT_p = bpool("wT", 2)
    r_p = bpool("rp", 4)
    boff_p = bpool("boff", 2)
    ab_p = bpool("ab", 8)
    diag_p = bpool("diag", 3)
    stg_sb = [sb("stg%d" % i, [128, 520], BF16) for i in range(3)]
    stg = Pool_([T(stg_sb[i], "stg%d" % i) for i in range(3)])
    small_sb = [sb("small%d" % i, [128, 64], F32) for i in range(8)]
    smalls = Pool_([T(small_sb[i], "small%d" % i) for i in range(8)])
    tm_sb = [sb("tm%d" % i, [128, 16], F32) for i in range(8)]
    tm = [T(tm_sb[i], "tm%d" % i) for i in range(8)]
    ps_h = [nc.alloc_psum_tensor("ps%d" % i, [128, 512], F32) for i in range(8)]
    psum_m = Pool_([T(ps_h[i], "ps%d" % i) for i in range(5)])
    psum_f = Pool_([T(ps_h[i], "ps%d" % i) for i in range(5, 8)])
    MP = (psum_m, f32s, b16s)
    FP = (psum_f, f32f, b16f)

    dbg_names = []

    def dbg(name, ts_, ap):
        if debug_out is None or name in dbg_names:
            return
        shp = list(ap.shape)
        n = 1
        for v_ in shp[1:]:
            n *= v_
        d = nc.dram_tensor("dbg_" + name, shp, ap.dtype, kind="ExternalOutput").ap()
        sem = P.new_dma_sem("dbg_" + name)
        P.op("sp", lambda: nc.sync.dma_start(out=d, in_=ap), reads=ts_, dma_sem=sem)
        dbg_names.append(name)

    def mm(out_t, out_ap, lhsT, rhs, start, stop, reads):
        return P.op("pe", lambda: nc.tensor.matmul(out_ap, lhsT, rhs, start=start, stop=stop),
                    reads=reads, writes=[out_t])

    def tr(out_t, out_ap, in_ap, ident, reads):
        return P.op("pe", lambda: nc.tensor.transpose(out_ap, in_ap, ident), reads=reads, writes=[out_t])

    def act(out_t, out_ap, in_ap, func, reads, bias=None, scale=None, eng="act"):
        def f():
            kw = {}
            if bias is not None:
                kw["bias"] = bias
            if scale is not None:
                kw["scale"] = scale
            return nc.scalar.activation(out_ap, in_ap, func, **kw)
        return P.op("act", f, reads=reads, writes=[out_t])

    def veng(e):
        return nc.vector if e == "dve" else nc.gpsimd

    def tt(e, out_t, out_ap, a, b, op, reads):
        return P.op(e, lambda: veng(e).tensor_tensor(out_ap, a, b, op), reads=reads, writes=[out_t])

    def ts(e, out_t, out_ap, a, s1, s2, op0, op1, reads):
        if op1 is None:
            return P.op(e, lambda: veng(e).tensor_scalar(out_ap, a, s1, None, op0), reads=reads, writes=[out_t])
        return P.op(e, lambda: veng(e).tensor_scalar(out_ap, a, s1, s2, op0, op1), reads=reads, writes=[out_t])

    def stt(e, out_t, out_ap, a, s, b, op0, op1, reads):
        w = out_t if isinstance(out_t, list) else [out_t]
        return P.op(e, lambda: veng(e).scalar_tensor_tensor(out_ap, a, s, b, op0, op1), reads=reads, writes=w)

    def cp(e, out_t, out_ap, in_ap, reads):
        if e == "act":
            return P.op("act", lambda: nc.scalar.copy(out_ap, in_ap), reads=reads, writes=[out_t])
        return P.op(e, lambda: veng(e).tensor_copy(out_ap, in_ap), reads=reads, writes=[out_t])

    s_par = P.new_dma_sem("par")
    P.op("sp", lambda: nc.sync.dma_start(out=par_sb[:, :], in_=par_d), writes=[par_t], dma_sem=s_par)
    s_cst = P.new_dma_sem("cst")
    P.op("sp", lambda: nc.sync.dma_start(out=cst_sb[:, :], in_=cst_d[:, 0:640]), writes=[cst_t], dma_sem=s_cst)
    cp("dve", cstb_t, cstb_sb[:, 0:128], identf, [cst_t])
    cp("dve", cstb_t, cstb_sb[:, 128:256], onesf, [cst_t])
    s_c2 = P.new_dma_sem("cst2")
    s_c3 = P.new_dma_sem("cst3")
    tmpa = f32s.get()
    tmpb = f32s.get()
    P.op("sp", lambda: nc.sync.dma_start(out=tmpa.ap[:, :], in_=cst_d[:, 640:1152]), writes=[tmpa], dma_sem=s_c2)
    P.op("sp", lambda: nc.sync.dma_start(out=tmpb.ap[:, :], in_=cst_d[:, 1152:1664]), writes=[tmpb], dma_sem=s_c3)
    cp("dve", cstb_t, cstb_sb[:, 256:768], tmpa.ap[:, :], [tmpa])
    cp("dve", cstb_t, cstb_sb[:, 768:1280], tmpb.ap[:, :], [tmpb])
    P.op("pool", lambda: nc.gpsimd.memset(misc_sb[:, 0:1], EPS), writes=[misc_t])
    P.op("pool", lambda: nc.gpsimd.memset(misc_sb[:, 1:2], -BIG), writes=[misc_t])
    P.op("pool", lambda: nc.gpsimd.memset(misc_sb[:, 2:3], 1.0), writes=[misc_t])
    P.op("pool", lambda: nc.gpsimd.memset(misc_sb[:, 3:4], float(np.log(128.0 ** -0.5))), writes=[misc_t])
    P.op("pool", lambda: nc.gpsimd.memset(misc_sb[:, 4:5], 0.0), writes=[misc_t])
    eps_ap = misc_sb[:, 0:1]
    nbig_ap = misc_sb[:, 1:2]
    one_ap = misc_sb[:, 2:3]
    lnq_ap = misc_sb[:, 3:4]
    for l in range(depth):
        P.op("pool", lambda l=l: nc.gpsimd.memset(S_sb[l][:, :], 0.0), writes=[S_t[l]])
        P.op("pool", lambda l=l: nc.gpsimd.memset(hist_sb[l][:, :], 0.0), writes=[hist_t[l]])
    for l in range(depth):
        a0 = poff(l) + OFF_ALOG
        act(negA_t, negA_sb[:, l * 4:(l + 1) * 4], par_sb[:, a0:a0 + 4], AF.Exp, [par_t])
    ts("dve", negA_t, negA_sb[:, :], negA_sb[:, :], -1.0, None, ALU.mult, None, [negA_t])

    conv_sem = {}
    conv_ops = {}
    s_wbd = P.new_dma_sem("wbd")
    for l in range(depth):
        P.op("pool", lambda l=l: nc.gpsimd.dma_start(
            out=wbd_sb[:, l * 64:(l + 1) * 64].rearrange("p (k n) -> p k n", k=KC),
            in_=w_in_d[l, :, 2048:2056].rearrange("(k p) n -> p k n", p=128)),
            writes=[wbd_t], dma_sem=s_wbd)

    def conv_dma(l, pi, col_off, src_ap, kch, ncols, pcols):
        t = wscr[l][pi]
        dst = t.ap.rearrange("p (k n) -> p k n", k=kch)[:, :, col_off:col_off + ncols]
        src = src_ap.rearrange("(k p) n -> p k n", p=128)
        key = (l, 0 if pi < 10 else (1 if pi < 21 else 2))
        if key not in conv_sem:
            conv_sem[key] = P.new_dma_sem("cv%d_%d" % key)
            conv_ops[key] = []
        o_ = P.op("pool", lambda: nc.gpsimd.dma_start(out=dst, in_=src), dma_sem=conv_sem[key])
        t.w = o_
        conv_ops[key].append(o_)

    for l in range(depth):
        for i_ in range(4):
            conv_dma(l, i_, 0, w_in_d[l, :, i_ * 512:(i_ + 1) * 512], KC, 512, 512)
        for g in range(4):
            for part in range(3):
                c0_ = 2056 + part * 512 + g * 128
                conv_dma(l, 4 + g, part * 128, w_in_d[l, :, c0_:c0_ + 128], KC, 128, 384)
        conv_dma(l, 8, 0, w_out_d[l, :, 0:512], KC, 512, 512)
        conv_dma(l, 9, 0, w_out_d[l, :, 512:1024], KC, 512, 512)
        for p in range(11):
            conv_dma(l, 10 + p, 0, wg_d[l, :, p * 256:(p + 1) * 256], KC, 256, 512)
            conv_dma(l, 10 + p, 256, wu_d[l, :, p * 256:(p + 1) * 256], KC, 256, 512)
        for hf in range(len(HFS)):
            r0_ = sum(HFS[:hf]) * 128
            for pn in range(2):
                conv_dma(l, 21 + hf * 2 + pn, 0, wd_d[l, r0_:r0_ + HFS[hf] * 128, pn * 512:(pn + 1) * 512],
                         HFS[hf], 512, 512)

    for key_, ops_ in conv_ops.items():
        fin_ = max(o_.semval for o_ in ops_)
        for o_ in ops_:
            o_.semval = fin_

    slot_ctr = [0, 0]

    def load_panel(l, pi):
        nm, kch, ncols, rg = specs[pi]
        si = slot_ctr[rg] % NSLOT
        slot_ctr[rg] += 1
        st = wslot[rg][si]
        src = wscr[l][pi]
        n = kch * ncols
        dst_sb = wslot_sb[rg][si]
        P.op("sp", lambda: nc.sync.dma_start(out=dst_sb[:, 0:n], in_=src.ap),
             reads=[src], writes=[st], dma_sem=wslot_sem[rg][si])
        view = dst_sb[:, 0:n].rearrange("p (k n) -> p k n", k=kch)
        return st, view

    def rms_stats(src_aps, src_ts, inv_d, eps_bias, pools, extra_bias=None):
        pp, fp_, bp = pools
        ps = pp.get()
        n = len(src_aps)
        for i in range(n):
            sq = bp.get()
            act(sq, sq.ap[:, :], src_aps[i], AF.Square, [src_ts[i]])
            mm(ps, ps.ap[:, :], onesb, sq.ap[:, :], i == 0, i == n - 1, [sq, cstb_t])
        rs = fp_.get()
        act(rs, rs.ap[:, :], ps.ap[:, :], AF.Ln, [ps, misc_t], bias=eps_bias, scale=inv_d)
        if extra_bias is None:
            act(rs, rs.ap[:, :], rs.ap[:, :], AF.Exp, [rs], scale=-0.5)
        else:
            act(rs, rs.ap[:, :], rs.ap[:, :], AF.Exp, [rs, misc_t], scale=-0.5, bias=extra_bias)
        return rs

    def rmsnorm_to_hT(goff, ln, pools):
        xres_sb, xres, hT_sb, hT = xresL_sb[ln], xresL[ln], hTL_sb[ln], hTL[ln]
        pp, fp_, bp = pools
        ps = pp.get()
        pend = None
        for i in range(KC + 1):
            cur = None
            if i < KC:
                sq = bp.get()
                act(sq, sq.ap[:, :], xres_sb[:, i, :], AF.Square, [xres[i]])
                cur = (i, sq)
            if pend is not None:
                pi_, psq = pend
                mm(ps, ps.ap[:, :], onesb, psq.ap[:, :], pi_ == 0, pi_ == KC - 1, [psq, cstb_t])
            pend = cur
            if i % 2 == 1:
                yield 0.3
        rs = fp_.get()
        act(rs, rs.ap[:, :], ps.ap[:, :], AF.Ln, [ps, misc_t], bias=eps_ap, scale=1.0 / D_MODEL)
        act(rs, rs.ap[:, :], rs.ap[:, :], AF.Exp, [rs], scale=-0.5)
        yield 0.3
        for k in range(KC):
            stt("dve", hT[k], hT_sb[:, k, :], xres_sb[:, k, :], par_sb[:, goff + k:goff + k + 1],
                rs.ap[:, :], ALU.mult, ALU.mult, [xres[k], rs, par_t])
        yield 0.5

    def proj_chunk(wt, wview, col0, rhs_list, rhs_ts, nk, pp):
        ps = pp.get()
        for k in range(nk):
            mm(ps, ps.ap[:, :], wview[:, k, col0:col0 + 128], rhs_list[k], k == 0, k == nk - 1,
               [wt, rhs_ts[k]])
        return ps


    def build_diag(l, d0, ntap):
        dt_ = diag_p.get()
        for j in range(ntap):
            col = poff(l) + OFF_DNC + d0 + j
            act(dt_, dt_.ap[:, j * 128:(j + 1) * 128], identb, AF.Copy, [cstb_t, par_t], scale=par_sb[:, col:col + 1])
        return dt_

    def mixer(l, ti, ln):
        xres_sb, xres, hT_sb, hT = xresL_sb[ln], xresL[ln], hTL_sb[ln], hTL[ln]
        hT_aps = [hT_sb[:, k, :] for k in range(KC)]
        psum = psum_m
        if l == 0:
            yield from load_x(ti, ln)
        yield from rmsnorm_to_hT(poff(l) + OFF_G1, ln, MP)
        delta_pre(l, ti, ln)
        yield 0.5
        hs = hist_sb[l]
        if l == 0 and ti == 0:
            dbg("xres0", xres, xres_sb[:, :, :])
            dbg("hT", hT, hT_sb[:, :, :])
        pendB = None
        pendCl = []
        order = list(range(16))
        wt = wv = None
        for it in range(16 + 3):
            curB = None
            if it < 16:
                gc_ = order[it]
                if gc_ % 4 == 0:
                    wt, wv = load_panel(l, gc_ // 4)
                ps = proj_chunk(wt, wv, (gc_ % 4) * 128, hT_aps, hT, KC, psum)
                if gc_ < 12:
                    st = stg.get()
                    cp("pool", st, st.ap[:, 0:3], hs[:, gc_ * 3:gc_ * 3 + 3], [hist_t[l]])
                    cp("act", st, st.ap[:, 3:3 + NT], ps.ap[:, :], [ps])
                    cp("pool", hist_t[l], hs[:, gc_ * 3:gc_ * 3 + 3], st.ap[:, NT:NT + 3], [st])
                    dg = build_diag(l, gc_ * 4, 4)
                    curB = (gc_, st, dg)
                else:
                    act(sz[gc_ - 12], sz_sb[:, gc_ - 12, :], ps.ap[:, :], AF.Silu, [ps])
            flush = []
            if len(pendCl) == 2:
                flush, pendCl = pendCl, []
            if pendB is not None:
                pg, pst, pdg = pendB
                ps2 = psum.get()
                for j in range(4):
                    mm(ps2, ps2.ap[:, :], pdg.ap[:, j * 128:(j + 1) * 128], pst.ap[:, j:j + NT], j == 0, j == 3, [pdg, pst])
                act(qkv[pg], qkv_sb[:, pg, :], ps2.ap[:, :], AF.Silu, [ps2])
                if pg < 8:
                    sq = b16s.get()
                    act(sq, sq.ap[:, :], qkv_sb[:, pg, :], AF.Square, [qkv[pg]])
                    pendCl.append((pg, sq))
            for (pg, psq) in flush:
                ps3 = psum.get()
                mm(ps3, ps3.ap[:, :], onesb, psq.ap[:, :], True, True, [psq, cstb_t])
                rs = f32s.get()
                act(rs, rs.ap[:, :], ps3.ap[:, :], AF.Ln, [ps3, misc_t], bias=eps_ap, scale=1.0)
                if pg < 4:
                    act(rs, rs.ap[:, :], rs.ap[:, :], AF.Exp, [rs, misc_t], scale=-0.5, bias=lnq_ap)
                else:
                    act(rs, rs.ap[:, :], rs.ap[:, :], AF.Exp, [rs], scale=-0.5)
                tt("dve", qkv[pg], qkv_sb[:, pg, :], qkv_sb[:, pg, :], rs.ap[:, :], ALU.mult, [qkv[pg], rs])
            pendB = curB
            yield 0.4
        if l == 0 and ti == 0:
            dbg("qkv", qkv, qkv_sb[:, :, :])
            dbg("sz", sz, sz_sb[:, :, :])
        yield from delta(l, ti, ln)
        def sc_A(g):
            wt, wv = load_panel(l, 4 + g)
            psB = proj_chunk(wt, wv, 0, hT_aps, hT, KC, psum)
            Bs = b16s.get()
            cp("act", Bs, Bs.ap[:, :], psB.ap[:, :], [psB])
            psC = proj_chunk(wt, wv, 128, hT_aps, hT, KC, psum)
            Cs = b16s.get()
            cp("act", Cs, Cs.ap[:, :], psC.ap[:, :], [psC])
            psH = proj_chunk(wt, wv, 256, hT_aps, hT, KC, psum)
            st = stg.get()
            h0 = 36 + g * 2
            cp("pool", st, st.ap[:, 0:2], hs[:, h0:h0 + 2], [hist_t[l]])
            tt("dve", st, st.ap[:, 2:2 + NT], psH.ap[:, :], Cs.ap[:, :], ALU.mult, [psH, Cs])
            cp("pool", hist_t[l], hs[:, h0:h0 + 2], st.ap[:, NT:NT + 2], [st])
            dg = build_diag(l, 48 + g * 3, 3)
            return {"g": g, "Bs": Bs, "st": st, "dg": dg}

        def sc_B(d):
            st, dg, Bs = d["st"], d["dg"], d["Bs"]
            psy = psum.get()
            for j in range(3):
                mm(psy, psy.ap[:, :], dg.ap[:, j * 128:(j + 1) * 128], st.ap[:, j:j + NT], j == 0, j == 2, [dg, st])
            ys = f32s.get()
            tt("dve", ys, ys.ap[:, :], psy.ap[:, :], Bs.ap[:, :], ALU.mult, [psy, Bs])
            sq = b16s.get()
            act(sq, sq.ap[:, :], ys.ap[:, :], AF.Square, [ys])
            d["ys"], d["sq"] = ys, sq

        def sc_C(d):
            g, ys, sq = d["g"], d["ys"], d["sq"]
            ps = psum.get()
            mm(ps, ps.ap[:, :], onesb, sq.ap[:, :], True, True, [sq, cstb_t])
            rs = f32s.get()
            act(rs, rs.ap[:, :], ps.ap[:, :], AF.Ln, [ps, misc_t], bias=eps_ap, scale=1.0 / 128.0)
            act(rs, rs.ap[:, :], rs.ap[:, :], AF.Exp, [rs], scale=-0.5)
            gcol = poff(l) + OFF_SCG + g
            stt("dve", ocat[4 + g], ocat_sb[:, 4 + g, :], ys.ap[:, :], par_sb[:, gcol:gcol + 1], rs.ap[:, :],
                ALU.mult, ALU.mult, [ys, rs, par_t])

        scd = {}
        for stg_i in range(6):
            if stg_i < 4:
                scd[stg_i] = sc_A(stg_i)
            if 0 <= stg_i - 1 < 4:
                sc_B(scd[stg_i - 1])
            if 0 <= stg_i - 2 < 4:
                sc_C(scd[stg_i - 2])
            yield 0.4
        if l == 0 and ti == 0:
            dbg("ocat", ocat, ocat_sb[:, 0:8, :])
        oc_aps = [ocat_sb[:, k, :] for k in range(8)]
        for oc in range(KC):
            if oc % 4 == 0:
                wt, wv = load_panel(l, 8 + oc // 4)
            ps = proj_chunk(wt, wv, (oc % 4) * 128, oc_aps, ocat, 8, psum)
            tt("dve", xres[oc], xres_sb[:, oc, :], ps.ap[:, :], xres_sb[:, oc, :], ALU.add, [ps, xres[oc]])
            yield 0.3

    def delta_pre(l, ti, ln):
        hT_sb, hT = hTL_sb[ln], hTL[ln]
        psum = psum_m
        psbd = psum.get()
        for ch in range(NCH):
            for k in range(KC):
                mm(psbd, psbd.ap[:, ch * 8:(ch + 1) * 8], hT_sb[:, k, ch * 128:(ch + 1) * 128],
                   wbd_sb[:, (l * KC + k) * 8:(l * KC + k + 1) * 8], k == 0, k == KC - 1, [hT[k], wbd_t])
        bd3 = psbd.ap[:, 0:NCH * 8].rearrange("p (c n) -> p c n", c=NCH)
        b_in = bd3[:, :, 0:4]
        a_in = bd3[:, :, 4:8]
        def v3(t):
            return t.ap[:, 0:NCH * 4].rearrange("p (c n) -> p c n", c=NCH)
        beta_t, g_t, eg_t, ed_t, egl_t, be_t, nbeta_t, tmp_t = tm
        act(tmp_t, v3(tmp_t), b_in, AF.Exp, [psbd], scale=-1.0)
        ts("dve", tmp_t, tmp_t.ap[:, :], tmp_t.ap[:, :], 1.0, None, ALU.add, None, [tmp_t])
        P.op("dve", lambda: nc.vector.reciprocal(beta_t.ap[:, :], tmp_t.ap[:, :]), reads=[tmp_t], writes=[beta_t])
        ts("dve", nbeta_t, nbeta_t.ap[:, :], beta_t.ap[:, :], -1.0, None, ALU.mult, None, [beta_t])
        y_t = smalls.get()
        dtb = par_sb[:, poff(l) + OFF_DTB:poff(l) + OFF_DTB + 4].unsqueeze(1).to_broadcast([128, NCH, 4])
        tt("dve", y_t, v3(y_t), a_in, dtb, ALU.add, [psbd, par_t])
        ay_t = smalls.get()
        ts("dve", ay_t, ay_t.ap[:, 0:16], y_t.ap[:, 0:16], -1.0, None, ALU.mult, None, [y_t])
        tt("dve", ay_t, ay_t.ap[:, 0:16], ay_t.ap[:, 0:16], y_t.ap[:, 0:16], ALU.max, [ay_t, y_t])
        act(ay_t, ay_t.ap[:, 0:16], ay_t.ap[:, 0:16], AF.Exp, [ay_t], scale=-1.0)
        act(ay_t, ay_t.ap[:, 0:16], ay_t.ap[:, 0:16], AF.Ln, [ay_t, misc_t], bias=one_ap, scale=1.0)
        ts("dve", y_t, y_t.ap[:, 0:16], y_t.ap[:, 0:16], 0.0, None, ALU.max, None, [y_t])
        tt("dve", y_t, y_t.ap[:, 0:16], y_t.ap[:, 0:16], ay_t.ap[:, 0:16], ALU.add, [y_t, ay_t])
        nA = negA_sb[:, l * 4:(l + 1) * 4].unsqueeze(1).to_broadcast([128, NCH, 4])
        tt("dve", g_t, v3(g_t), v3(y_t), nA, ALU.mult, [y_t, negA_t])
        psg = psum.get()
        mm(psg, psg.ap[:, 0:16], trif, g_t.ap[:, 0:16], True, True, [cst_t, g_t])
        mm(psg, psg.ap[:, 16:32], onesf, g_t.ap[:, 0:16], True, True, [cst_t, g_t])
        gcl = smalls.get()
        cp("dve", gcl, gcl.ap[:, 0:32], psg.ap[:, 0:32], [psg])
        tt("dve", gcl, gcl.ap[:, 32:48], gcl.ap[:, 16:32], gcl.ap[:, 0:16], ALU.subtract, [gcl])
        ex = smalls.get()
        act(ex, ex.ap[:, 0:48], gcl.ap[:, 0:48], AF.Exp, [gcl])
        cp("dve", eg_t, eg_t.ap[:, :], ex.ap[:, 0:16], [ex])
        cp("dve", egl_t, egl_t.ap[:, :], ex.ap[:, 16:32], [ex])
        cp("dve", ed_t, ed_t.ap[:, :], ex.ap[:, 32:48], [ex])
        tt("dve", be_t, be_t.ap[:, :], beta_t.ap[:, :], eg_t.ap[:, :], ALU.mult, [beta_t, eg_t])

    def delta(l, ti, ln):
        hT_sb, hT = hTL_sb[ln], hTL[ln]
        psum = psum_m
        beta_t, g_t, eg_t, ed_t, egl_t, be_t, nbeta_t, tmp_t = tm

        def bc(t, ch):
            return t.ap[:, ch * 4:(ch + 1) * 4].unsqueeze(2).to_broadcast([128, 4, 128])

        def h3(ap):
            return ap.rearrange("p (h c) -> p h c", h=4)

        if l == 0 and ti == 0:
            dbg("beta", [beta_t], beta_t.ap[:, :])
            dbg("g", [g_t], g_t.ap[:, :])
            dbg("eg", [eg_t], eg_t.ap[:, :])
            dbg("ed", [ed_t], ed_t.ap[:, :])
            dbg("egl", [egl_t], egl_t.ap[:, :])
        def hmm(ps_t, lhs_t, rhs_t):
            for h in range(H):
                hs_ = slice(h * 128, (h + 1) * 128)
                mm(ps_t, ps_t.ap[:, hs_], lhs_t.ap[:, hs_], rhs_t.ap[:, hs_], True, True, [lhs_t, rhs_t])

        def s_front(c):
            ch = c["ch"]
            cs = c["cs"]
            gts = f32s.get()
            for h in range(H):
                stt("dve", gts, gts.ap[:, h * 128:(h + 1) * 128], trif, g_t.ap[:, ch * 4 + h:ch * 4 + h + 1], bigif,
                    ALU.mult, ALU.add, [cst_t, g_t])
            ptk = psum.get()
            ptkb = ptk.ap[:, :].bitcast(BF16)[:, 0:512]
            for h in range(H):
                tr(ptk, ptkb[:, h * 128:(h + 1) * 128], qkv_sb[:, 4 + h, cs], identb, [qkv[4 + h], cstb_t])
            ptv = psum.get()
            ptvb = ptv.ap[:, :].bitcast(BF16)[:, 0:512]
            for h in range(H):
                tr(ptv, ptvb[:, h * 128:(h + 1) * 128], qkv_sb[:, 8 + h, cs], identb, [qkv[8 + h], cstb_t])
            Ru = r_p.get()
            Rw = r_p.get()
            kdec = kdec_p.get()
            tt("dve", Ru, h3(Ru.ap[:, :]), h3(ptvb), bc(beta_t, ch), ALU.mult, [ptv, beta_t])
            tt("dve", Rw, h3(Rw.ap[:, :]), h3(ptkb), bc(be_t, ch), ALU.mult, [ptk, be_t])
            tt("dve", kdec, h3(kdec.ap[:, :]), h3(ptkb), bc(ed_t, ch), ALU.mult, [ptk, ed_t])
            c["Ru"], c["Rw"], c["kdec"] = Ru, Rw, kdec
            c["gts"] = gts

        def s_front2(c):
            ch = c["ch"]
            cs = c["cs"]
            gts = c["gts"]
            psD = psum.get()
            for h in range(H):
                mm(psD, psD.ap[:, h * 128:(h + 1) * 128], gts.ap[:, h * 128:(h + 1) * 128], smf, True, True,
                   [gts, cst_t])
            psG = psum.get()
            mm(psG, psG.ap[:, :], onesf, gts.ap[:, :], True, True, [gts, cst_t])
            decS = dec_p.get()
            act(decS, decS.ap[:, :], psD.ap[:, :], AF.Exp, [psD, misc_t], bias=nbig_ap, scale=1.0)
            egrow = dec_p.get()
            act(egrow, egrow.ap[:, :], psG.ap[:, :], AF.Exp, [psG, misc_t], bias=nbig_ap, scale=1.0)
            decI = dec_p.get()
            tt("pool", decI, decI.ap[:, :], decS.ap[:, :], i4b, ALU.add, [decS, cstb_t])
            psKK = psum.get()
            for h in range(H):
                mm(psKK, psKK.ap[:, h * 128:(h + 1) * 128], qkv_sb[:, 4 + h, cs], qkv_sb[:, 4 + h, cs], True, True,
                   [qkv[4 + h]])
            B0 = b16s.get()
            for h in range(H):
                hs_ = slice(h * 128, (h + 1) * 128)
                stt("dve", B0, B0.ap[:, hs_], psKK.ap[:, hs_], nbeta_t.ap[:, ch * 4 + h:ch * 4 + h + 1],
                    decS.ap[:, hs_], ALU.mult, ALU.mult, [psKK, nbeta_t, decS])
            Bk = ab_p.get()
            tt("pool", Bk, Bk.ap[:, :], B0.ap[:, :], bd4b, ALU.mult, [B0, cstb_t])
            Boff = boff_p.get()
            tt("pool", Boff, Boff.ap[:, :], B0.ap[:, :], Bk.ap[:, :], ALU.subtract, [B0, Bk])
            psQK = psum.get()
            for h in range(H):
                mm(psQK, psQK.ap[:, h * 128:(h + 1) * 128], qkv_sb[:, h, cs], qkv_sb[:, 4 + h, cs], True, True,
                   [qkv[h], qkv[4 + h]])
            qk = b16s.get()
            tt("dve", qk, qk.ap[:, :], psQK.ap[:, :], decI.ap[:, :], ALU.mult, [psQK, decI])
            qdec = qdec_p.get()
            tt("pool", qdec, h3(qdec.ap[:, :]), qkv_sb[:, 0:4, cs], h3(egrow.ap[:, :]), ALU.mult,
               [qkv[0], qkv[1], qkv[2], qkv[3], egrow])
            c["Bk"], c["Boff"], c["qk"], c["qdec"] = Bk, Boff, qk, qdec

        def s_tr(c):
            Bk, qk = c["Bk"], c["qk"]
            pta = psum.get()
            ptab = pta.ap[:, :].bitcast(BF16)
            for h in range(H):
                tr(pta, ptab[:, h * 128:(h + 1) * 128], Bk.ap[:, h * 128:(h + 1) * 128], identb, [Bk, cstb_t])
            for h in range(H):
                tr(pta, ptab[:, 512 + h * 128:512 + (h + 1) * 128], qk.ap[:, h * 128:(h + 1) * 128], identb,
                   [qk, cstb_t])
            Ak = ab_p.get()
            qkT = qkT_p.get()
            cp("act", Ak, Ak.ap[:, :], ptab[:, 0:512], [pta])
            cp("act", qkT, qkT.ap[:, :], ptab[:, 512:1024], [pta])
            Pk = chain_p.get()
            Qk = chain_p.get()
            tt("pool", Pk, Pk.ap[:, :], Ak.ap[:, :], i4b, ALU.add, [Ak, cstb_t])
            tt("pool", Qk, Qk.ap[:, :], Bk.ap[:, :], i4b, ALU.add, [Bk, cstb_t])
            c["Ak"], c["qkT"], c["Pk"], c["Qk"] = Ak, qkT, Pk, Qk

        def s_h1(c, lev):
            Ak, Bk = c["Ak"], c["Bk"]
            psA = psum.get()
            hmm(psA, Bk, Ak)
            A2 = ab_p.get()
            cp("act", A2, A2.ap[:, :], psA.ap[:, :], [psA])
            c["A2"] = A2
            if lev < 4:
                psB = psum.get()
                hmm(psB, Ak, Bk)
                B2 = ab_p.get()
                cp("act", B2, B2.ap[:, :], psB.ap[:, :], [psB])
                c["B2"] = B2

        def s_h2(c, lev):
            Pk, Qk, A2 = c["Pk"], c["Qk"], c["A2"]
            psP = psum.get()
            hmm(psP, Qk, A2)
            P2 = chain_p.get()
            tt("dve", P2, P2.ap[:, :], psP.ap[:, :], Pk.ap[:, :], ALU.add, [psP, Pk])
            psQ = psum.get()
            hmm(psQ, A2, Qk)
            Q2 = chain_p.get()
            tt("dve", Q2, Q2.ap[:, :], psQ.ap[:, :], Qk.ap[:, :], ALU.add, [psQ, Qk])
            c["Pk"], c["Qk"], c["Ak"] = P2, Q2, A2
            if lev < 4:
                c["Bk"] = c["B2"]

        def s_fin1(c):
            psGm = psum.get()
            hmm(psGm, c["Boff"], c["Pk"])
            Gm = b16s.get()
            cp("act", Gm, Gm.ap[:, :], psGm.ap[:, :], [psGm])
            c["Gm"] = Gm

        def s_fin2(c):
            Pk = c["Pk"]
            psT = psum.get()
            hmm(psT, c["Qk"], c["Gm"])
            TT = b16s.get()
            tt("dve", TT, TT.ap[:, :], psT.ap[:, :], Pk.ap[:, :], ALU.add, [psT, Pk])
            c["TT"] = TT

        def s_fin3(c):
            TT = c["TT"]
            psu = psum.get()
            hmm(psu, TT, c["Ru"])
            u = f32s.get()
            cp("act", u, u.ap[:, :], psu.ap[:, :], [psu])
            psw = psum.get()
            hmm(psw, c["Rw"], TT)
            wT = wT_p.get()
            cp("act", wT, wT.ap[:, :], psw.ap[:, :], [psw])
            c["u"], c["wT"] = u, wT

        def s_rec1(c):
            u, wT = c["u"], c["wT"]
            Sb = Sb_sb[l]
            psws = psum.get()
            for h in range(H):
                hs_ = slice(h * 128, (h + 1) * 128)
                mm(psws, psws.ap[:, hs_], wT.ap[:, hs_], Sb[:, hs_], True, True, [wT, Sb_t[l]])
            vnew = b16s.get()
            tt("dve", vnew, vnew.ap[:, :], u.ap[:, :], psws.ap[:, :], ALU.subtract, [u, psws])
            c["vnew"] = vnew

        def s_rec2(c):
            ch, cs = c["ch"], c["cs"]
            qdec, qkT, kdec, vnew = c["qdec"], c["qkT"], c["kdec"], c["vnew"]
            Sb = Sb_sb[l]
            pso = psum.get()
            for h in range(H):
                hs_ = slice(h * 128, (h + 1) * 128)
                mm(pso, pso.ap[:, hs_], Sb[:, hs_], qdec.ap[:, hs_], True, False, [Sb_t[l], qdec])
                mm(pso, pso.ap[:, hs_], vnew.ap[:, hs_], qkT.ap[:, hs_], False, True, [vnew, qkT])
            psS = psum.get()
            for h in range(H):
                hs_ = slice(h * 128, (h + 1) * 128)
                mm(psS, psS.ap[:, hs_], kdec.ap[:, hs_], vnew.ap[:, hs_], True, True, [kdec, vnew])
            for h in range(H):
                hs_ = slice(h * 128, (h + 1) * 128)
                stt("dve", S_t[l], S_sb[l][:, hs_], S_sb[l][:, hs_], egl_t.ap[:, ch * 4 + h:ch * 4 + h + 1],
                    psS.ap[:, hs_], ALU.mult, ALU.add, [S_t[l], egl_t, psS])
            cp("act", Sb_t[l], Sb[:, :], S_sb[l][:, :], [S_t[l]])
            sq = b16s.get()
            act(sq, sq.ap[:, :], pso.ap[:, :], AF.Square, [pso])
            c["pso"], c["sq"] = pso, sq

        def s_rec3(c):
            cs, pso, sq = c["cs"], c["pso"], c["sq"]
            ps = psum.get()
            mm(ps, ps.ap[:, :], onesb, sq.ap[:, :], True, True, [sq, cstb_t])
            rs = f32s.get()
            act(rs, rs.ap[:, :], ps.ap[:, :], AF.Ln, [ps, misc_t], bias=eps_ap, scale=1.0 / 128.0)
            act(rs, rs.ap[:, :], rs.ap[:, :], AF.Exp, [rs], scale=-0.5)
            on = f32s.get()
            tt("dve", on, on.ap[:, :], pso.ap[:, :], rs.ap[:, :], ALU.mult, [pso, rs])
            gcol = poff(l) + OFF_DNG
            stt("dve", [ocat[0], ocat[1], ocat[2], ocat[3]], ocat_sb[:, 0:4, cs], h3(on.ap[:, :]),
                par_sb[:, gcol:gcol + 1], sz_sb[:, 0:4, cs],
                ALU.mult, ALU.mult, [on, par_t, sz[0], sz[1], sz[2], sz[3]])

        cp("act", Sb_t[l], Sb_sb[l][:, :], S_sb[l][:, :], [S_t[l]])
        yield 1.3
        for pr in range(NCH // 2):
            cc = [{"ch": pr * 2 + i, "cs": slice((pr * 2 + i) * 128, (pr * 2 + i + 1) * 128)} for i in range(2)]
            for c in cc:
                s_front(c)
            yield 1.3
            for c in cc:
                s_front2(c)
            yield 1.3
            yield 1.0
            for c in cc:
                s_tr(c)
            yield 1.3
            for lev in range(6):
                for c in cc:
                    if lev >= 1:
                        s_h2(c, lev - 1)
                    if lev < 5:
                        s_h1(c, lev)
                yield 1.3
            for c in cc:
                s_fin1(c)
            yield 1.3
            for c in cc:
                s_fin2(c)
            yield 1.3
            for c in cc:
                s_fin3(c)
            yield 1.3
            s_rec1(cc[0])
            yield 1.3
            s_rec2(cc[0])
            yield 1.3
            s_rec1(cc[1])
            s_rec3(cc[0])
            yield 1.3
            s_rec2(cc[1])
            yield 1.3
            s_rec3(cc[1])
            yield 1.0

    def ffn(l, ti, ln):
        xres_sb, xres, hT_sb, hT = xresL_sb[ln], xresL[ln], hTL_sb[ln], hTL[ln]
        hT_aps = [hT_sb[:, k, :] for k in range(KC)]
        psum = psum_f
        if l == 0 and ti == 0:
            dbg("xres1", xres, xres_sb[:, :, :])
        yield from rmsnorm_to_hT(poff(l) + OFF_G2, ln, FP)
        act_aps = [act_sb[:, k, :] for k in range(HF)]

        def down_half(hf):
            for pn in range(2):
                wt, wv = load_panel(l, 21 + hf * 2 + pn)
                for c in range(4):
                    oc = pn * 4 + c
                    ps = proj_chunk(wt, wv, c * 128, act_aps, actT, HFS[hf], psum)
                    tt("dve", xres[oc], xres_sb[:, oc, :], ps.ap[:, :], xres_sb[:, oc, :], ALU.add, [ps, xres[oc]])
                    yield

        for p in range(11):
            wt, wv = load_panel(l, 10 + p)
            for c in range(2):
                fc = p * 2 + c
                psg_ = proj_chunk(wt, wv, c * 128, hT_aps, hT, KC, psum)
                psu_ = proj_chunk(wt, wv, 256 + c * 128, hT_aps, hT, KC, psum)
                sg = b16f.get()
                act(sg, sg.ap[:, :], psg_.ap[:, :], AF.Silu, [psg_])
                tt("dve", actT[fc % HF], act_sb[:, fc % HF, :], psu_.ap[:, :], sg.ap[:, :], ALU.mult, [psu_, sg])
                yield
            if p == 3:
                yield from down_half(0)
            elif p == 7:
                yield from down_half(1)
        yield from down_half(2)
        if l == depth - 1:
            yield from store_out(ti, ln)

    def load_x(ti, ln):
        xres_sb, xres = xresL_sb[ln], xresL[ln]
        psum = psum_m
        for blk in range(NT // 128):
            r0 = ti * NT + blk * 128
            for half in range(2):
                i = xin.i % NXI
                xt = xin.get()
                P.op("pool", lambda xt=xt, r0=r0, half=half: nc.gpsimd.dma_start(
                    out=xt.ap[:, :], in_=x_d[r0:r0 + 128, half * 512:(half + 1) * 512]),
                    writes=[xt], dma_sem=xin_sem[i])
                ps = psum.get()
                for kk in range(4):
                    k = half * 4 + kk
                    tr(ps, ps.ap[:, kk * 128:(kk + 1) * 128], xt.ap[:, kk * 128:(kk + 1) * 128], identf, [xt, cst_t])
                for kk in range(4):
                    k = half * 4 + kk
                    cp("dve", xres[k], xres_sb[:, k, blk * 128:(blk + 1) * 128], ps.ap[:, kk * 128:(kk + 1) * 128], [ps])
            yield 0.3

    def store_out(ti, ln):
        xres_sb, xres = xresL_sb[ln], xresL[ln]
        psum = psum_f
        rs = rms_stats([xres_sb[:, k, :] for k in range(KC)], xres, 1.0 / D_MODEL, eps_ap, FP)
        for k in range(KC):
            stt("dve", xres[k], xres_sb[:, k, :], xres_sb[:, k, :], par_sb[:, OFF_GF + k:OFF_GF + k + 1],
                rs.ap[:, :], ALU.mult, ALU.mult, [xres[k], rs, par_t])
        yield
        for blk in range(NT // 128):
            r0 = ti * NT + blk * 128
            for half in range(2):
                i = xout.i % NXB
                xo = xout.get()
                ps = psum.get()
                for kk in range(4):
                    k = half * 4 + kk
                    tr(ps, ps.ap[:, kk * 128:(kk + 1) * 128], xres_sb[:, k, blk * 128:(blk + 1) * 128], identf,
                       [xres[k], cst_t])
                cp("act", xo, xo.ap[:, :], ps.ap[:, :], [ps])
                P.op("sp", lambda xo=xo, r0=r0, half=half: nc.sync.dma_start(
                    out=out_d[r0:r0 + 128, half * 512:(half + 1) * 512], in_=xo.ap[:, :]),
                    reads=[xo], dma_sem=xout_sem[i])
            yield

    EST = {"M": 70.0, "F": 50.0}
    lane_phases = [[], []]
    for ti in range(ntiles):
        ln = ti % 2
        for l in range(depth):
            lane_phases[ln].append(("M", mixer(l, ti, ln)))
            lane_phases[ln].append(("F", ffn(l, ti, ln)))
    lane_phases[1] = [None] + lane_phases[1]
    nslots = max(len(lane_phases[0]), len(lane_phases[1]))
    for s_ in range(nslots):
        cur = []
        for ln in range(2):
            if s_ < len(lane_phases[ln]) and lane_phases[ln][s_] is not None:
                kind, gen = lane_phases[ln][s_]
                cur.append([gen, EST[kind], 0])
        while cur:
            cur.sort(key=lambda e: (e[2] + 1.0) / e[1])
            e = cur[0]
            try:
                w_ = next(e[0])
                e[2] += (1.0 if w_ is None else w_)
            except StopIteration:
                cur.remove(e)

    print("sbuf bytes remaining/partition:", nc.sbuf_bytes_remaining, flush=True)
    P.emit()
    return nc


def _layout_params(inp, depth):
    cols = []
    for l in range(depth):
        cols.append(inp["norm1_g"][l].reshape(8, 128).T)
        cols.append(inp["norm2_g"][l].reshape(8, 128).T)
        cols.append(inp["dn_conv_w"][l].reshape(4, 12, 128).transpose(2, 1, 0).reshape(128, 48))
        cols.append(inp["sc_conv_w"][l].reshape(3, 4, 128).transpose(2, 1, 0).reshape(128, 12))
        cols.append(inp["dn_norm_g"][l].reshape(128, 1))
        cols.append(inp["sc_norm_g"][l].reshape(4, 128).T)
        cols.append(np.broadcast_to(inp["dn_a_log"][l][None, :], (128, 4)))
        cols.append(np.broadcast_to(inp["dn_dt_bias"][l][None, :], (128, 4)))
    cols.append(inp["final_norm_g"].reshape(8, 128).T)
    return np.ascontiguousarray(np.concatenate(cols, axis=1).astype(np.float32))


def _consts():
    j = np.arange(128)
    ident = np.eye(128, dtype=np.float32)
    tri = (j[:, None] <= j[None, :]).astype(np.float32)
    sm = (j[:, None] > j[None, :]).astype(np.float32)
    bigi = ident * BIG
    ones = np.ones((128, 128), np.float32)
    i4 = np.tile(ident, (1, 4))
    bd = ((j[:, None] // 64) == (j[None, :] // 64)).astype(np.float32)
    bd4 = np.tile(bd, (1, 4))
    return np.ascontiguousarray(np.concatenate([ident, tri, sm, bigi, ones, i4, bd4], axis=1))


_NC_CACHE = {}
_LAST = {}


def run(inputs, seq, depth, ncores, debug_out=None):
    key = (seq, depth)
    if key not in _NC_CACHE:
        _NC_CACHE[key] = build_program(seq, depth, debug_out)
    nc = _NC_CACHE[key]
    params = _layout_params(inputs, depth)
    consts = _consts()
    in_maps = []
    for c in range(ncores):
        in_maps.append({
            "x": np.ascontiguousarray(inputs["x"][c, :seq]),
            "w_in": np.ascontiguousarray(inputs["w_in"][:depth]),
            "w_out": np.ascontiguousarray(inputs["w_out"][:depth]),
            "ffn_w_gate": np.ascontiguousarray(inputs["ffn_w_gate"][:depth]),
            "ffn_w_up": np.ascontiguousarray(inputs["ffn_w_up"][:depth]),
            "ffn_w_down": np.ascontiguousarray(inputs["ffn_w_down"][:depth]),
            "params": params,
            "consts": consts,
        })
    res = run_bass_kernel_spmd(nc, in_maps, core_ids=list(range(ncores)))
    _LAST["res"] = res.results
    return np.stack([res.results[c]["out"] for c in range(ncores)], axis=0)


def kernel(**inputs):
    inputs = {k: np.asarray(v) for k, v in inputs.items()}
    out = run(inputs, SEQ, DEPTH, BATCH)
    return out.astype(np.float32)
```
